# Optimizing a Trainium2 kernel written in Bass

```python
import math
import jax, jax.numpy as jnp
from jax import lax
import numpy as np

D_MODEL = 2048
BATCH = 4
SEQ = 4096
DEPTH = 1

CTX_LEN = 256
GRID_W = 64
N_HEADS = 16
HEAD_DIM = 128
N_KV_HEADS = 4
AXIS_DIM = HEAD_DIM // 2
Q_BLOCK = 128
ROPE_THETA = 10000.0
HYENA_WIDTH = D_MODEL // 2
SHORT_CONV = 3
FILTER_HIDDEN = 64
FILTER_EMB = 17
FILTER_BANDS = (FILTER_EMB - 1) // 2
DECAY_TARGET = 1e-2
FAST_DECAY_PCT = 0.3
SLOW_DECAY_PCT = 1.5
MIN_DECAY = math.log(DECAY_TARGET) / SLOW_DECAY_PCT
MAX_DECAY = math.log(DECAY_TARGET) / FAST_DECAY_PCT
D_FF = 4 * D_MODEL
EPS = 1e-6
Q_W = N_HEADS * HEAD_DIM
KV_W = N_KV_HEADS * HEAD_DIM
Q_OFF = 3 * HYENA_WIDTH
K_OFF = Q_OFF + Q_W
V_OFF = K_OFF + KV_W
GA_OFF = V_OFF + KV_W
GB_OFF = GA_OFF + D_MODEL
IN_W = GB_OFF + D_MODEL

kernel_name = "hybrid_hyena_gqa_dit_block"


def rmsnorm(x, g):
    xf = x.astype(jnp.float32)
    y = xf * lax.rsqrt(jnp.mean(xf * xf, axis=-1, keepdims=True) + EPS)
    return (y * g.astype(jnp.float32)).astype(x.dtype)


def modulate(h, shift, scale):
    return h * (1.0 + scale) + shift


def adaln(cond, w, b):
    return jax.nn.silu(cond) @ w + b


def axial_rope_tables(rows_count):
    rows = jnp.repeat(jnp.arange(rows_count), GRID_W)
    cols = jnp.tile(jnp.arange(GRID_W), rows_count)
    inv = ROPE_THETA ** (-jnp.arange(0, AXIS_DIM, 2, dtype=jnp.float32) / AXIS_DIM)
    ang = jnp.stack([rows[:, None] * inv, cols[:, None] * inv], axis=1)
    return jnp.cos(ang), jnp.sin(ang)


def apply_rope(x, cos, sin):
    B, L, H, Dh = x.shape
    xr = x.astype(jnp.float32).reshape(B, L, H, 2, AXIS_DIM)
    x1, x2 = xr[..., :AXIS_DIM // 2], xr[..., AXIS_DIM // 2:]
    c = cos[None, :, None]
    s = sin[None, :, None]
    out = jnp.concatenate([x1 * c - x2 * s, x2 * c + x1 * s], axis=-1)
    return out.reshape(B, L, H, Dh).astype(x.dtype)


def short_conv(u, w, b):
    up = jnp.pad(u, ((0, 0), (1, 1), (0, 0)))
    return up[:, :-2] * w[0] + up[:, 1:-1] * w[1] + up[:, 2:] * w[2] + b


def hyena_filter_spectrum(L, w1, b1, w2, b2, w3, b3, w4, freq):
    f32 = jnp.float32
    C = w4.shape[-1] // 2
    t = jnp.linspace(0.0, 1.0, L, dtype=f32)[:, None]
    wpos = 2.0 * math.pi * jnp.arange(L, dtype=f32)[:, None] / L
    bands = jnp.linspace(1e-4, FILTER_BANDS - 1, FILTER_BANDS, dtype=f32)
    emb = jnp.concatenate([t, jnp.cos(bands * wpos), -jnp.sin(bands * wpos)], axis=-1)
    fr = freq.astype(f32)
    h = jnp.sin(fr * (emb @ w1.astype(f32) + b1.astype(f32)))
    h = jnp.sin(fr * (h @ w2.astype(f32) + b2.astype(f32)))
    h = jnp.sin(fr * (h @ w3.astype(f32) + b3.astype(f32)))
    h = h @ w4.astype(f32)
    deltas = jnp.abs(jnp.linspace(MIN_DECAY, MAX_DECAY, C, dtype=f32))
    h = h.reshape(L, 2, C) * jnp.exp(-t * deltas)[:, None, :]
    k = jnp.concatenate([h[:, 0], jnp.zeros((1, C), f32), h[:0:-1, 1]], axis=0)
    k = k / jnp.sum(jnp.abs(k), axis=0, keepdims=True)
    return jnp.fft.rfft(k, axis=0)


def hyena_branch(u, conv_w, conv_b, fw1, fb1, fw2, fb2, fw3, fb3, fw4, ffreq, fbias):
    B, L, _ = u.shape
    u = short_conv(u, conv_w, conv_b)
    x0, x1, v = jnp.split(u, 3, axis=-1)
    kf = hyena_filter_spectrum(L, fw1, fb1, fw2, fb2, fw3, fb3, fw4, ffreq)
    z = (x1 * v).astype(jnp.float32)
    y = jnp.fft.irfft(jnp.fft.rfft(z, n=2 * L, axis=1) * kf[None], n=2 * L, axis=1)[:, :L]
    y = y + z * fbias.astype(jnp.float32)
    return x0 * y.astype(u.dtype)


def block_attention(q, k, v):
    B, L, H, Dh = q.shape
    KVH = k.shape[2]
    G = H // KVH
    nblk = L // Q_BLOCK
    qb = q.reshape(B, nblk, Q_BLOCK, KVH, G, Dh).transpose(1, 0, 2, 3, 4, 5)
    scale = Dh ** -0.5

    def one_block(qi):
        s = jnp.einsum('bqkgd,bskd->bkgqs', qi, k, preferred_element_type=jnp.float32) * scale
        p = jax.nn.softmax(s, axis=-1).astype(v.dtype)
        return jnp.einsum('bkgqs,bskd->bqkgd', p, v)

    o = lax.map(one_block, qb)
    return o.transpose(1, 0, 2, 3, 4, 5).reshape(B, L, H * Dh)


def context_kv(hc, w_in_l, k_gain):
    B, Lc, _ = hc.shape
    kv = hc @ w_in_l[:, K_OFF:GA_OFF]
    kc = rmsnorm(kv[..., :KV_W].reshape(B, Lc, N_KV_HEADS, HEAD_DIM), k_gain)
    vc = kv[..., KV_W:].reshape(B, Lc, N_KV_HEADS, HEAD_DIM)
    return kc, vc


def token_mixer(proj, ctx_keys, ctx_values, rope_cos, rope_sin, latent,
                conv_w, conv_b, fw1, fb1, fw2, fb2, fw3, fb3, fw4, ffreq, fbias,
                q_gain, k_gain, w_ba, w_bb, w_o):
    B, L, _ = proj.shape
    y_a = hyena_branch(proj[..., :Q_OFF], conv_w, conv_b, fw1, fb1, fw2, fb2, fw3, fb3, fw4, ffreq, fbias)
    q = rmsnorm(proj[..., Q_OFF:K_OFF].reshape(B, L, N_HEADS, HEAD_DIM), q_gain)
    k = rmsnorm(proj[..., K_OFF:V_OFF].reshape(B, L, N_KV_HEADS, HEAD_DIM), k_gain)
    v = proj[..., V_OFF:GA_OFF].reshape(B, L, N_KV_HEADS, HEAD_DIM)
    if latent:
        q = apply_rope(q, rope_cos, rope_sin)
        k = apply_rope(k, rope_cos, rope_sin)
        k = jnp.concatenate([k, ctx_keys], axis=1)
        v = jnp.concatenate([v, ctx_values], axis=1)
    y_b = block_attention(q, k, v)
    gate_a = jax.nn.sigmoid(proj[..., GA_OFF:GB_OFF])
    gate_b = jax.nn.sigmoid(proj[..., GB_OFF:])
    merged = gate_a * (y_a @ w_ba) + gate_b * (y_b @ w_bb)
    return merged @ w_o


def sq_relu_mlp(h, w1, w2):
    return jnp.square(jax.nn.relu(h @ w1)) @ w2


def setup_inputs(seed: int = 0) -> dict:
    key = jax.random.key(seed)
    ks = jax.random.split(key, 25)
    f32 = jnp.float32
    C = HYENA_WIDTH

    def nrm(k, shape, scale):
        return jax.random.normal(k, shape, f32) * scale

    return {
        "x": nrm(ks[0], (BATCH, SEQ, D_MODEL), 1.0),
        "c": nrm(ks[1], (BATCH, D_MODEL), 1.0),
        "ctx": nrm(ks[2], (BATCH, CTX_LEN, D_MODEL), 1.0),
        "c_ctx": nrm(ks[3], (D_MODEL,), 1.0),
        "w_ada": nrm(ks[4], (DEPTH, D_MODEL, 6 * D_MODEL), 0.5 * D_MODEL ** -0.5),
        "b_ada": nrm(ks[5], (DEPTH, 6 * D_MODEL), 0.02),
        "norm_gains": 1.0 + nrm(ks[6], (DEPTH, 4, D_MODEL), 0.05),
        "w_in": nrm(ks[7], (DEPTH, D_MODEL, IN_W), D_MODEL ** -0.5),
        "conv_w": nrm(ks[8], (DEPTH, SHORT_CONV, 3 * C), SHORT_CONV ** -0.5),
        "conv_b": nrm(ks[9], (DEPTH, 3 * C), 0.02),
        "filt_w1": nrm(ks[10], (DEPTH, FILTER_EMB, FILTER_HIDDEN), FILTER_EMB ** -0.5),
        "filt_b1": nrm(ks[11], (DEPTH, FILTER_HIDDEN), 0.1),
        "filt_w2": nrm(ks[12], (DEPTH, FILTER_HIDDEN, FILTER_HIDDEN), FILTER_HIDDEN ** -0.5),
        "filt_b2": nrm(ks[13], (DEPTH, FILTER_HIDDEN), 0.1),
        "filt_w3": nrm(ks[14], (DEPTH, FILTER_HIDDEN, FILTER_HIDDEN), FILTER_HIDDEN ** -0.5),
        "filt_b3": nrm(ks[15], (DEPTH, FILTER_HIDDEN), 0.1),
        "filt_w4": nrm(ks[16], (DEPTH, FILTER_HIDDEN, 2 * C), FILTER_HIDDEN ** -0.5),
        "filt_freq": 1.0 + nrm(ks[17], (DEPTH, FILTER_HIDDEN), 0.1),
        "filt_bias": nrm(ks[18], (DEPTH, C), 1.0),
        "qk_gains": 1.0 + nrm(ks[19], (DEPTH, 2, HEAD_DIM), 0.05),
        "w_branch_a": nrm(ks[20], (DEPTH, C, D_MODEL), C ** -0.5),
        "w_branch_b": nrm(ks[21], (DEPTH, Q_W, D_MODEL), Q_W ** -0.5),
        "w_out": nrm(ks[22], (DEPTH, D_MODEL, D_MODEL), D_MODEL ** -0.5),
        "w_ff1": nrm(ks[23], (DEPTH, D_MODEL, D_FF), D_MODEL ** -0.5),
        "w_ff2": nrm(ks[24], (DEPTH, D_FF, D_MODEL), D_FF ** -0.5),
    }


def reference(x, c, ctx, c_ctx, w_ada, b_ada, norm_gains, w_in, conv_w, conv_b,
              filt_w1, filt_b1, filt_w2, filt_b2, filt_w3, filt_b3, filt_w4, filt_freq, filt_bias,
              qk_gains, w_branch_a, w_branch_b, w_out, w_ff1, w_ff2):
    L = x.shape[1]
    ROWS = L // GRID_W
    rope_cos, rope_sin = axial_rope_tables(ROWS)
    xc = ctx
    for l in range(DEPTH):
        update_ctx = l < DEPTH - 1
        mod = adaln(c, w_ada[l], b_ada[l])[:, None, :]
        mod_c = adaln(c_ctx, w_ada[l], b_ada[l])[None, None, :]
        sh1, sc1, gt1, sh2, sc2, gt2 = jnp.split(mod, 6, axis=-1)
        csh1, csc1, cgt1, csh2, csc2, cgt2 = jnp.split(mod_c, 6, axis=-1)
        g_pre1, g_post1, g_pre2, g_post2 = norm_gains[l]
        mixer_w = (conv_w[l], conv_b[l], filt_w1[l], filt_b1[l], filt_w2[l], filt_b2[l],
                   filt_w3[l], filt_b3[l], filt_w4[l], filt_freq[l], filt_bias[l],
                   qk_gains[l, 0], qk_gains[l, 1], w_branch_a[l], w_branch_b[l], w_out[l])

        h = modulate(rmsnorm(x, g_pre1), sh1, sc1)
        hc = modulate(rmsnorm(xc, g_pre1), csh1, csc1)
        kc, vc = context_kv(hc, w_in[l], qk_gains[l, 1])
        mix = token_mixer(h @ w_in[l], kc, vc, rope_cos, rope_sin, True, *mixer_w)
        x = x + gt1 * rmsnorm(mix, g_post1)
        if update_ctx:
            mix_c = token_mixer(hc @ w_in[l], None, None, None, None, False, *mixer_w)
            xc = xc + cgt1 * rmsnorm(mix_c, g_post1)

        h2 = modulate(rmsnorm(x, g_pre2), sh2, sc2)
        x = x + gt2 * rmsnorm(sq_relu_mlp(h2, w_ff1[l], w_ff2[l]), g_post2)
        if update_ctx:
            h2c = modulate(rmsnorm(xc, g_pre2), csh2, csc2)
            xc = xc + cgt2 * rmsnorm(sq_relu_mlp(h2c, w_ff1[l], w_ff2[l]), g_post2)
    return x
```

```python
import math
from contextlib import ExitStack
import numpy as np
import ml_dtypes
import concourse.bass as bass
import concourse.mybir as mybir
from concourse.bass_utils import run_bass_kernel_spmd

F32 = mybir.dt.float32
BF16 = mybir.dt.bfloat16
I32 = mybir.dt.int32
AF = mybir.ActivationFunctionType
ALU = mybir.AluOpType
AX = mybir.AxisListType

D = 2048; L = 4096; NB = 4; CTX = 256; HALF = 2048
C = 1024; INW = 10240; DFF = 8192
Q_OFF = 3072; K_OFF = 5120; V_OFF = 5632; GA_OFF = 6144; GB_OFF = 8192
NKEY = L + CTX
EPS = 1e-6
STOP_AFTER = 99
DEBUG = False


class Buf:
    def __init__(self, name, dram=False):
        self.name = name; self.w = None; self.r = {}; self.dram = dram


class Trk:
    def __init__(self, nc, es):
        self.nc = nc; self.es = es
        self.eng = {'pe': nc.tensor, 'act': nc.scalar, 'dve': nc.vector, 'pool': nc.gpsimd, 'sp': nc.sync}
        self.sem = {}; self.cnt = {}
        for e in ('pe', 'act', 'dve', 'pool'):
            self.sem[e] = es.enter_context(nc.semaphore("s_" + e)); self.cnt[e] = 0
        self.waited = {}
        self.dsem = {}; self.dcnt = {}
        self.free_dsems = []
        self.all_dma_toks = {}

    def wait(self, e, tok):
        if tok is None:
            return
        key, val = tok
        if key == e and e == 'pe':
            return
        if self.waited.get((e, key), 0) >= val:
            return
        self.waited[(e, key)] = val
        sem = self.sem[key] if key in self.sem else self.dsem[key]
        self.eng[e].wait_ge(sem, val)

    def _pre(self, e, reads, writes):
        for b in reads:
            self.wait(e, b.w)
        for b in writes:
            self.wait(e, b.w)
            for kk_, vv_ in list(b.r.items()):
                self.wait(e, (kk_, vv_))

    def _post(self, tok, reads, writes):
        for b in writes:
            b.w = tok; b.r = {}
        for b in reads:
            if b not in writes:
                b.r[tok[0]] = max(b.r.get(tok[0], 0), tok[1])

    def op(self, e, fn, reads=(), writes=()):
        self._pre(e, reads, writes)
        ins = fn(self.eng[e])
        self.cnt[e] += 1
        ins.then_inc(self.sem[e], 1)
        tok = (e, self.cnt[e])
        self._post(tok, reads, writes)
        return tok

    def dma(self, q, out, in_, reads=(), writes=(), key=None):
        self._pre(q, reads, writes)
        if key is None:
            srcs = [b for b in reads if not b.dram]
            key = ("st_" + srcs[0].name) if (writes[0].dram and srcs) else writes[0].name
        if key not in self.dsem:
            self.dsem[key] = self.es.enter_context(self.nc.semaphore("d_" + key)); self.dcnt[key] = 0
        self.dcnt[key] += 16
        self.eng[q].dma_start(out=out, in_=in_).then_inc(self.dsem[key], 16)
        tok = (key, self.dcnt[key])
        self.all_dma_toks[key] = tok
        self._post(tok, reads, writes)
        return tok

    def alias(self, to_list, from_list):
        for t in to_list:
            for f in from_list:
                if f.w is not None:
                    t.r[f.w[0]] = max(t.r.get(f.w[0], 0), f.w[1])
                for kk_, vv_ in f.r.items():
                    t.r[kk_] = max(t.r.get(kk_, 0), vv_)

    def barrier(self):
        toks = [(e, self.cnt[e]) for e in ('pe', 'act', 'dve', 'pool') if self.cnt[e] > 0]
        toks += list(self.all_dma_toks.values())
        for e in ('pe', 'act', 'dve', 'pool', 'sp'):
            for t in toks:
                if t[0] == e:
                    continue
                self.wait(e, t)


def build_nc():
    nc = bass.Bass("TRN2", target_bir_lowering=False)

    def din(name, shape, dt=F32):
        return nc.dram_tensor(name, list(shape), dt, kind="ExternalInput").ap()

    def dscr(name, shape, dt=BF16):
        return nc.dram_tensor(name, list(shape), dt, kind="Internal").ap()

    xp = din("xp", [L, D]); ctx = din("ctx", [CTX, D]); cvec = din("cvec", [128, 16, 2])
    w_ada = din("w_ada", [D + 1, 6 * D])[0:D, :]; bada = din("bada", [128, 96]); rowb = din("rowb", [129, 4, D])[0:128]
    gpre = din("gpre", [128, 2, 16]); w_in = din("w_in", [D + 1, INW])[0:D, :]
    convw = din("convw", [128, 24, 3]); convb = din("convb", [128, 24])
    embT = din("embT", [17, 2 * L]); fw1 = din("fw1", [17, 64]); fvec = din("fvec", [128, 4])
    fw2 = din("fw2", [64, 64]); fw3d = din("fw3d", [64, 128]); fw4 = din("fw4", [128, C])
    negt = din("negt", [128, 64]); delta = din("delta", [128, C]); fbias = din("fbias", [128, 8])
    qkg = din("qkg", [128, 2, 128]); ropec = din("ropec", [128, 32, 128], BF16); ropes = din("ropes", [128, 32, 128], BF16)
    F1d = din("F1d", [64, 130], BF16); F1k = din("F1k", [128, 130], BF16)
    T3 = din("T3", [129, 65, 4, 64], BF16)[0:128]; E64 = din("E64", [128, 128], BF16); Rt = din("Rt", [65, 64, 2, 32], BF16)
    ident_d = din("ident", [128, 128], BF16); halo = din("halo", [128, 2])
    w_ba = din("w_ba", [C + 1, D])[0:C, :]; w_bb = din("w_bb", [D + 1, D])[0:D, :]; w_o = din("w_o", [D + 1, D])[0:D, :]
    w_ff1 = din("w_ff1", [D + 1, DFF])[0:D, :]; w_ff2 = din("w_ff2", [DFF + 1, D])[0:DFF, :]
    out = nc.dram_tensor("out", [HALF, D], F32, kind="ExternalOutput").ap()

    uT = dscr("uT", [3 * C, L])
    gT = dscr("gT", [2 * D, HALF])
    qT = dscr("qT", [16, 128, HALF])
    kT = dscr("kT", [4, 128, NKEY])
    vtok = dscr("vtok", [NKEY, 512])
    yaT = dscr("yaT", [C, HALF])
    ybT = dscr("ybT", [D, HALF])
    x1s = dscr("x1s", [HALF, D], F32)
    wf1b = dscr("wf1b", [D, DFF]); wf2b = dscr("wf2b", [DFF, D])
    winb = dscr("winb", [D, INW]); wbab = dscr("wbab", [C, D]); wbbb = dscr("wbbb", [D, D]); wob = dscr("wob", [D, D])
    dbg = {}
    if DEBUG:
        for nm, shp, dt in (("d_mod", [128, 96, 2], F32), ("d_hid", [128, 2 * L], BF16), ("d_uT", [3 * C, L], BF16),
                            ("d_qT", [16, 128, HALF], BF16), ("d_kT", [4, 128, NKEY], BF16), ("d_v", [NKEY, 512], BF16),
                            ("d_gT", [2 * D, HALF], BF16), ("d_ya", [C, HALF], BF16), ("d_yb", [D, HALF], BF16),
                            ("d_x1", [HALF, D], F32)):
            dbg[nm] = nc.dram_tensor(nm, shp, dt, kind="ExternalOutput").ap()

    with ExitStack() as es:
        T = Trk(nc, es)
        sb = lambda name, shape, dt=F32, st=es: st.enter_context(nc.sbuf_tensor("sb_" + name, list(shape), dt))
        pf = [es.enter_context(nc.psum_tensor("pf%d" % i, [128, 512], F32)) for i in range(5)]
        pb = [es.enter_context(nc.psum_tensor("pb%d" % i, [128, 1024], BF16)) for i in range(3)]
        pfB = [Buf("pf%d" % i) for i in range(5)]; pbB = [Buf("pb%d" % i) for i in range(3)]
        pfi = [0]; pbi = [0]

        def npf():
            i = pfi[0] % 5; pfi[0] += 1; return pf[i], pfB[i]

        def npb():
            i = pbi[0] % 3; pbi[0] += 1; return pb[i], pbB[i]

        ident = sb("ident", [128, 128], BF16); identB = Buf("ident")
        T.dma('sp', ident[:], ident_d, writes=[identB])
        ones_f = sb("ones_f", [128, 128]); onesB = Buf("ones_f")
        T.op('pool', lambda e: e.memset(ones_f[:], 1.0), writes=[onesB])
        ones_b = sb("ones_b", [128, 128], BF16); onesbB = Buf("ones_b")
        T.op('pool', lambda e: e.memset(ones_b[:], 1.0), writes=[onesbB])
        modv = sb("modv", [128, 6, 16]); modB = Buf("modv")
        Grow = sb("Grow", [128, 2, D]); GrowB = Buf("Grow")
        epsT = sb("epsT", [128, 1]); epsB = Buf("epsT")
        T.op('pool', lambda e: e.memset(epsT[:], EPS), writes=[epsB])

        sH = ExitStack()
        hid = sb("hid", [128, 2 * L], BF16, sH); hidB = Buf("hid")
        wf1B = Buf("wf1b", dram=True); wf2B = Buf("wf2b", dram=True)
        s2b = ExitStack()
        if True:
            cb = [sb("cb%d" % i, [128, 8192], BF16, s2b) for i in range(2)]; cbB = [Buf("cb%d" % i) for i in range(2)]
            it = 0
            winB = Buf("winb", dram=True); wmB = Buf("wmerge", dram=True)
            for k in range(16):
                for hcol in range(2):
                    i = it % 2; it += 1
                    T.dma('pool', cb[i][:, 0:5120], w_in[k * 128:(k + 1) * 128, hcol * 5120:(hcol + 1) * 5120], writes=[cbB[i]])
                    T.dma('sp', winb[k * 128:(k + 1) * 128, hcol * 5120:(hcol + 1) * 5120], cb[i][:, 0:5120], reads=[cbB[i]], writes=[winB])
            for src_, dst_, nk in ((w_ba, wbab, 2), (w_bb, wbbb, 4), (w_o, wob, 4)):
                for k in range(nk):
                    i = it % 2; it += 1
                    T.dma('pool', cb[i][:].rearrange("p (a c) -> p a c", a=4),
                          src_[k * 512:(k + 1) * 512, :].rearrange("(a p) c -> p a c", p=128), writes=[cbB[i]])
                    T.dma('sp', dst_[k * 512:(k + 1) * 512, :].rearrange("(a p) c -> p a c", p=128),
                          cb[i][:].rearrange("p (a c) -> p a c", a=4), reads=[cbB[i]], writes=[wmB])
            for k in range(16):
                i = it % 2; it += 1
                T.dma('pool', cb[i][:], w_ff1[k * 128:(k + 1) * 128, :], writes=[cbB[i]])
                T.dma('sp', wf1b[k * 128:(k + 1) * 128, :], cb[i][:], reads=[cbB[i]], writes=[wf1B])
            for k in range(16):
                i = it % 2; it += 1
                T.dma('pool', cb[i][:].rearrange("p (a c) -> p a c", a=4),
                      w_ff2[k * 512:(k + 1) * 512, :].rearrange("(a p) c -> p a c", p=128), writes=[cbB[i]])
                T.dma('sp', wf2b[k * 512:(k + 1) * 512, :].rearrange("(a p) c -> p a c", p=128),
                      cb[i][:].rearrange("p (a c) -> p a c", a=4), reads=[cbB[i]], writes=[wf2B])

        with ExitStack() as s1:
            s1.enter_context(nc.named_scope("s1"))
            cv = sb("cv", [128, 16, 2], F32, s1); cvB = Buf("cv")
            T.dma('act', cv[:], cvec, writes=[cvB])
            sl = sb("sl", [128, 16, 2], F32, s1); slB = Buf("sl")
            T.op('act', lambda e: e.activation(out=sl[:], in_=cv[:], func=AF.Silu), reads=[cvB], writes=[slB])
            bad = sb("bad", [128, 96], F32, s1); badB = Buf("bad")
            T.dma('act', bad[:], bada, writes=[badB])
            gp = sb("gp", [128, 2, 16], F32, s1); gpB = Buf("gp")
            T.dma('act', gp[:], gpre, writes=[gpB])
            rb = sb("rb", [128, 4, D], F32, s1); rbB = Buf("rb")
            T.dma('act', rb[:], rowb, writes=[rbB])
            modT = sb("modT", [128, 96, 2], F32, s1); modTB = Buf("modT")
            identf = sb("identf", [128, 128], F32, s1); identfB = Buf("identf")
            T.op('dve', lambda e: e.tensor_copy(out=identf[:], in_=ident[:]), reads=[identB], writes=[identfB])
            wcb = [sb("wcb%d" % i, [128, 16, 256], F32, s1) for i in range(2)]; wcbB = [Buf("wcb%d" % i) for i in range(2)]
            mrow = sb("mrow", [2, 6 * D], F32, s1); mrowB = Buf("mrow")
            for cb_ in range(48):
                w_, wB_ = wcb[cb_ % 2], wcbB[cb_ % 2]
                T.dma('act', w_[:], w_ada[:, cb_ * 256:(cb_ + 1) * 256].rearrange("(k p) c -> p k c", p=128), writes=[wB_])
                ps, psB = npf()

                def f(e, ps=ps, w_=w_):
                    for k in range(16):
                        ins = e.matmul(ps[0:2, 0:256], lhsT=sl[:, k, :], rhs=w_[:, k, :], start=(k == 0), stop=(k == 15))
                    return ins
                T.op('pe', f, reads=[wB_, slB], writes=[psB])
                T.op('dve', lambda e, ps=ps, cb_=cb_: e.tensor_copy(out=mrow[:, cb_ * 256:(cb_ + 1) * 256], in_=ps[0:2, 0:256]),
                     reads=[psB], writes=[mrowB])
            ps, psB = npf()

            def ft(e, ps=ps):
                for j in range(96):
                    ins = e.transpose(ps[:, 2 * j:2 * j + 2], mrow[0:2, j * 128:(j + 1) * 128], identf[0:2, 0:2])
                return ins
            T.op('pe', ft, reads=[mrowB, identfB], writes=[psB])
            T.op('dve', lambda e, ps=ps: e.tensor_tensor(
                out=modT[:], in0=ps[:, 0:192].rearrange("p (j t) -> p j t", t=2),
                in1=bad[:].unsqueeze(2).to_broadcast([128, 96, 2]), op=ALU.add), reads=[psB, badB], writes=[modTB])
            for hfc in range(2):
                base = 2 * D if hfc == 0 else 5 * D
                for bl in range(4):
                    pr, prB = npf()
                    T.op('pe', lambda e, pr=pr, base=base, bl=bl: e.matmul(
                        pr[:], lhsT=ones_f[0:1, :], rhs=mrow[0:1, base + bl * 512:base + (bl + 1) * 512], start=True, stop=True),
                        reads=[mrowB, onesB], writes=[prB])
                    T.op('dve', lambda e, pr=pr, bl=bl, hfc=hfc: e.tensor_tensor(
                        out=Grow[:, hfc, bl * 512:(bl + 1) * 512], in0=pr[:], in1=rb[:, hfc, bl * 512:(bl + 1) * 512],
                        op=ALU.add), reads=[prB, rbB], writes=[GrowB])
                T.op('dve', lambda e, hfc=hfc: e.tensor_tensor(out=Grow[:, hfc, :], in0=Grow[:, hfc, :],
                                                              in1=rb[:, 2 + hfc, :], op=ALU.mult),
                     reads=[rbB], writes=[GrowB])
            T.op('dve', lambda e: e.scalar_tensor_tensor(out=modv[:, 0, :], in0=modT[:, 16:32, 0], scalar=1.0,
                                                         in1=gp[:, 0, :], op0=ALU.add, op1=ALU.mult),
                 reads=[modTB, gpB], writes=[modB])
            T.op('dve', lambda e: e.tensor_copy(out=modv[:, 1, :], in_=modT[:, 0:16, 0]), reads=[modTB], writes=[modB])
            T.op('dve', lambda e: e.scalar_tensor_tensor(out=modv[:, 2, :], in0=modT[:, 16:32, 1], scalar=1.0,
                                                         in1=gp[:, 0, :], op0=ALU.add, op1=ALU.mult),
                 reads=[modTB, gpB], writes=[modB])
            T.op('dve', lambda e: e.tensor_copy(out=modv[:, 3, :], in_=modT[:, 0:16, 1]), reads=[modTB], writes=[modB])
            T.op('dve', lambda e: e.scalar_tensor_tensor(out=modv[:, 4, :], in0=modT[:, 64:80, 0], scalar=1.0,
                                                         in1=gp[:, 1, :], op0=ALU.add, op1=ALU.mult),
                 reads=[modTB, gpB], writes=[modB])
            T.op('dve', lambda e: e.tensor_copy(out=modv[:, 5, :], in_=modT[:, 48:64, 0]), reads=[modTB], writes=[modB])
            if DEBUG:
                T.dma('act', dbg["d_mod"], modT[:], reads=[modTB], writes=[Buf("dbg_mod", dram=True)])
            T.barrier()
        if STOP_AFTER <= 1:
            sH.close()
            return nc, T

        with ExitStack() as s2:
            s2.enter_context(nc.named_scope("s2"))
            emb = sb("emb", [17, 2 * L], F32, s2); embB = Buf("emb")
            T.dma('act', emb[:], embT, writes=[embB])
            w1 = sb("w1", [17, 64], F32, s2); w2 = sb("w2", [64, 64], F32, s2); w3 = sb("w3", [64, 128], F32, s2)
            fv = sb("fv", [128, 4], F32, s2); fwB = Buf("fw")
            T.dma('act', w1[:], fw1, writes=[fwB]); T.dma('act', w2[:], fw2, writes=[fwB])
            T.dma('act', w3[:], fw3d, writes=[fwB]); T.dma('act', fv[:], fvec, writes=[fwB])
            fsc = sb("fsc", [128, 4], F32, s2); fscB = Buf("fsc")
            T.op('dve', lambda e: e.tensor_scalar(out=fsc[:, 3:4], in0=fv[:, 3:4], scalar1=1.0 / (2 * math.pi), scalar2=None,
                                                  op0=ALU.mult), reads=[fwB], writes=[fscB])
            T.op('dve', lambda e: e.tensor_scalar(out=fsc[:, 0:3], in0=fv[:, 0:3], scalar1=fsc[:, 3:4], scalar2=None,
                                                  op0=ALU.mult), reads=[fwB, fscB], writes=[fscB])
            h1 = sb("h1", [64, 2 * L], F32, s2); h1B = Buf("h1")
            h2 = sb("h2", [64, 2 * L], F32, s2); h2B = Buf("h2")
            tv = [sb("tv%d" % i, [128, 512], F32, s2) for i in range(2)]; tvB = [Buf("tv%d" % i) for i in range(2)]
            ti = [sb("ti%d" % i, [128, 512], I32, s2) for i in range(2)]; tiB = [Buf("ti%d" % i) for i in range(2)]
            tf = [sb("tf%d" % i, [128, 512], F32, s2) for i in range(2)]; tfB = [Buf("tf%d" % i) for i in range(2)]
            it = 0
            for layer in range(3):
                src, srcB, wgt, kk, mm = ((emb, embB, w1, 17, 64), (h1, h1B, w2, 64, 64), (h2, h2B, w3, 64, 128))[layer]
                for ch in range(16):
                    ps, psB = npf()
                    T.op('pe', lambda e, ps=ps, src=src, wgt=wgt, kk=kk, mm=mm, ch=ch: e.matmul(
                        ps[0:mm, :], lhsT=wgt[0:kk, 0:mm], rhs=src[0:kk, ch * 512:(ch + 1) * 512], start=True, stop=True),
                        reads=[srcB, fwB], writes=[psB])
                    i = it % 2; it += 1
                    T.op('dve', lambda e, ps=ps, i=i, mm=mm, layer=layer: e.tensor_scalar(
                        out=tv[i][0:mm, :], in0=ps[0:mm, :], scalar1=fsc[0:mm, 3:4], scalar2=fsc[0:mm, layer:layer + 1],
                        op0=ALU.mult, op1=ALU.add), reads=[psB, fscB], writes=[tvB[i]])
                    T.op('dve', lambda e, i=i, mm=mm: e.tensor_copy(out=ti[i][0:mm, :], in_=tv[i][0:mm, :]),
                         reads=[tvB[i]], writes=[tiB[i]])
                    T.op('dve', lambda e, i=i, mm=mm: e.tensor_copy(out=tf[i][0:mm, :], in_=ti[i][0:mm, :]),
                         reads=[tiB[i]], writes=[tfB[i]])
                    T.op('dve', lambda e, i=i, mm=mm: e.tensor_tensor(out=tv[i][0:mm, :], in0=tv[i][0:mm, :],
                                                                       in1=tf[i][0:mm, :], op=ALU.subtract),
                         reads=[tfB[i]], writes=[tvB[i]])
                    dst, dstB = ((h1, h1B), (h2, h2B), (hid, hidB))[layer]
                    T.op('act', lambda e, i=i, mm=mm, dst=dst, ch=ch: e.activation(
                        out=dst[0:mm, ch * 512:(ch + 1) * 512], in_=tv[i][0:mm, :], func=AF.Sin, scale=2 * math.pi),
                        reads=[tvB[i]], writes=[dstB])
            T.op('pool', lambda e: e.memset(hid[0:64, L:2 * L], 0.0), writes=[hidB])
            T.op('pool', lambda e: e.memset(hid[64:128, 0:L + 1], 0.0), writes=[hidB])
            if DEBUG:
                T.dma('act', dbg["d_hid"], hid[:], reads=[hidB], writes=[Buf("dbg_hid", dram=True)])
            T.barrier()
        s2b.close()
        if STOP_AFTER <= 2:
            sH.close()
            return nc, T

        uTB = Buf("uT", dram=True); gTB = Buf("gT", dram=True); qTB = Buf("qT", dram=True); kTB = Buf("kT", dram=True); vB = Buf("vtok", dram=True)
        with ExitStack() as s3:
            s3.enter_context(nc.named_scope("s3"))
            rc = sb("rc", [128, 32, 128], BF16, s3); rs = sb("rs", [128, 32, 128], BF16, s3); ropeB = Buf("rope")
            T.dma('sp', rc[:], ropec, writes=[ropeB]); T.dma('sp', rs[:], ropes, writes=[ropeB])
            qg = sb("qg", [128, 2, 128], F32, s3); qgB = Buf("qg")
            T.dma('sp', qg[:], qkg, writes=[qgB])
            hT = sb("hT", [128, 16, 1024], BF16, s3); hTB = Buf("hT")
            xt = [sb("xt%d" % i, [128, D], F32, s3) for i in range(2)]; xtB = [Buf("xt%d" % i) for i in range(2)]
            xn = [sb("xn%d" % i, [128, D], BF16, s3) for i in range(2)]; xnB = [Buf("xn%d" % i) for i in range(2)]
            junk = sb("junk", [128, D], BF16, s3); junkB = Buf("junk")
            st1 = [sb("st1_%d" % i, [128, 4], F32, s3) for i in range(2)]; st1B = [Buf("st1_%d" % i) for i in range(2)]
            wt = [sb("wt%d" % i, [128, 16, 512], BF16, s3) for i in range(3)]; wtB = [Buf("wt%d" % i) for i in range(3)]
            cst = [sb("cst%d" % i, [128, 1024], BF16, s3) for i in range(2)]; cstB = [Buf("cst%d" % i) for i in range(2)]
            qs = [sb("qs%d" % i, [128, 512], F32, s3) for i in range(4)]; qsB = [Buf("qs%d" % i) for i in range(4)]
            qsq = sb("qsq", [128, 512], F32, s3); qsqB = Buf("qsq")
            qn = [sb("qn%d" % i, [128, 512], F32, s3) for i in range(4)]; qnB = [Buf("qn%d" % i) for i in range(4)]
            qr = [sb("qr%d" % i, [128, 512], F32, s3) for i in range(4)]; qrB = [Buf("qr%d" % i) for i in range(4)]
            qb = [sb("qb%d" % i, [128, 512], BF16, s3) for i in range(4)]; qbB = [Buf("qb%d" % i) for i in range(4)]
            stq = [sb("stq%d" % i, [128, 4], F32, s3) for i in range(4)]; stqB = [Buf("stq%d" % i) for i in range(4)]
            qtt = [sb("qtt%d" % i, [128, 4, 128], BF16, s3) for i in range(4)]; qttB = [Buf("qtt%d" % i) for i in range(4)]
            ctr = {'x': 0, 'wk': 0, 'wt': 0, 'cst': 0, 'q': 0}

            def build_hT(src_ap, ntiles, gi, si):
                for tl in range(ntiles):
                    i = ctr['x'] % 2; ctr['x'] += 1
                    T.dma('sp', xt[i][:], src_ap[tl * 128:(tl + 1) * 128, :], writes=[xtB[i]])
                    T.op('act', lambda e, i=i: e.activation(out=junk[:], in_=xt[i][:], func=AF.Square,
                                                            accum_out=st1[i][:, 0:1]),
                         reads=[xtB[i]], writes=[junkB, st1B[i]])
                    T.op('act', lambda e, i=i: e.activation(out=st1[i][:, 1:2], in_=st1[i][:, 0:1], func=AF.Sqrt,
                                                            bias=epsT[:, 0:1], scale=1.0 / D),
                         reads=[epsB], writes=[st1B[i]])
                    T.op('dve', lambda e, i=i: e.reciprocal(out=st1[i][:, 2:3], in_=st1[i][:, 1:2]), writes=[st1B[i]])
                    T.op('dve', lambda e, i=i: e.tensor_scalar(out=xn[i][:], in0=xt[i][:], scalar1=st1[i][:, 2:3],
                                                              scalar2=None, op0=ALU.mult),
                         reads=[xtB[i], st1B[i]], writes=[xnB[i]])
                    for hb in range(2):
                        p_, pB_ = npb()

                        def f(e, p_=p_, i=i, hb=hb):
                            for j in range(8):
                                k = hb * 8 + j
                                ins = e.transpose(p_[:, j * 128:(j + 1) * 128], xn[i][:, k * 128:(k + 1) * 128], ident[:])
                            return ins
                        T.op('pe', f, reads=[xnB[i], identB], writes=[pB_])
                        for j in range(8):
                            k = hb * 8 + j
                            T.op('act', lambda e, p_=p_, j=j, k=k, tl=tl: e.activation(
                                out=hT[:, k, tl * 128:(tl + 1) * 128], in_=p_[:, j * 128:(j + 1) * 128], func=AF.Identity,
                                bias=modv[:, si, k:k + 1], scale=modv[:, gi, k:k + 1]),
                                reads=[pB_, modB], writes=[hTB])

            pre = {}

            def preload(col):
                i = ctr['wt'] % 3; ctr['wt'] += 1
                T.dma('sp', wt[i][:], winb[:, col:col + 512].rearrange("(k p) c -> p k c", p=128), reads=[winB], writes=[wtB[i]])
                pre.setdefault(col, []).append(i)

            def getw(col):
                if not pre.get(col):
                    preload(col)
                return pre[col].pop(0)

            def chan_major4(col, ntok, dst_fn, dstB, sig=False):
                i = getw(col)
                for q in range(4):
                    ci = ctr['cst'] % 2; ctr['cst'] += 1
                    for sbk in range(ntok // 512):
                        ps, psB = npf()

                        def f(e, ps=ps, i=i, sbk=sbk, q=q):
                            for k in range(16):
                                ins = e.matmul(ps[:], lhsT=wt[i][:, k, q * 128:(q + 1) * 128], rhs=hT[:, k, sbk * 512:(sbk + 1) * 512],
                                               start=(k == 0), stop=(k == 15))
                            return ins
                        T.op('pe', f, reads=[wtB[i], hTB], writes=[psB])
                        T.op('act', lambda e, ps=ps, ci=ci, sbk=sbk: e.activation(
                            out=cst[ci][:, sbk * 512:(sbk + 1) * 512], in_=ps[:], func=(AF.Sigmoid if sig else AF.Copy)),
                            reads=[psB], writes=[cstB[ci]])
                    T.dma('sp', dst_fn(q), cst[ci][:, 0:ntok], reads=[cstB[ci]], writes=[dstB])

            def tok_major(col, ntiles, kind, tok0, gidx=0, rope=True):
                i = getw(col)
                for tl in range(ntiles):
                    ps, psB = npf()

                    def f(e, ps=ps, i=i, tl=tl):
                        for k in range(16):
                            ins = e.matmul(ps[:], lhsT=hT[:, k, tl * 128:(tl + 1) * 128], rhs=wt[i][:, k, :],
                                           start=(k == 0), stop=(k == 15))
                        return ins
                    T.op('pe', f, reads=[wtB[i], hTB], writes=[psB])
                    j = ctr['q'] % 4; ctr['q'] += 1
                    t0 = tok0 + tl * 128
                    if kind == 'v':
                        T.op('act', lambda e, ps=ps, j=j: e.activation(out=qb[j][:], in_=ps[:], func=AF.Copy),
                             reads=[psB], writes=[qbB[j]])
                        T.dma('sp', vtok[t0:t0 + 128, :], qb[j][:], reads=[qbB[j]], writes=[vB])
                        continue
                    T.op('act', lambda e, ps=ps, j=j: e.activation(out=qs[j][:], in_=ps[:], func=AF.Copy),
                         reads=[psB], writes=[qsB[j]])
                    T.op('dve', lambda e, j=j: e.tensor_tensor(out=qsq[:], in0=qs[j][:], in1=qs[j][:], op=ALU.mult),
                         reads=[qsB[j]], writes=[qsqB])
                    sj = stq[j]
                    T.op('dve', lambda e, sj=sj: e.tensor_reduce(out=sj[:, 0:4], in_=qsq[:].rearrange("p (h d) -> p h d", d=128),
                                                                 axis=AX.X, op=ALU.add), reads=[qsqB], writes=[stqB[j]])
                    T.op('act', lambda e, sj=sj: e.activation(out=sj[:, 0:4], in_=sj[:, 0:4], func=AF.Sqrt,
                                                              bias=epsT[:, 0:1], scale=1.0 / 128),
                         reads=[epsB], writes=[stqB[j]])
                    T.op('dve', lambda e, sj=sj: e.reciprocal(out=sj[:, 0:4], in_=sj[:, 0:4]), writes=[stqB[j]])
                    gsel = 0 if kind == 'q' else 1
                    for h in range(4):
                        T.op('dve', lambda e, j=j, h=h, sj=sj: e.scalar_tensor_tensor(
                            out=qn[j][:, h * 128:(h + 1) * 128], in0=qs[j][:, h * 128:(h + 1) * 128], scalar=sj[:, h:h + 1],
                            in1=qg[:, gsel, :], op0=ALU.mult, op1=ALU.mult),
                            reads=[qsB[j], stqB[j], qgB], writes=[qnB[j]])
                    if rope:
                        ti_ = t0 // 128
                        qn4 = qn[j][:].rearrange("p (h d) -> p h d", d=128)
                        qr4 = qr[j][:].rearrange("p (h d) -> p h d", d=128)
                        cb_ = rc[:, ti_, :].unsqueeze(1).to_broadcast([128, 4, 128])
                        qn5 = qn[j][:].rearrange("p (h a t d) -> p h a t d", a=2, t=2, d=32)
                        qr5 = qr[j][:].rearrange("p (h a t d) -> p h a t d", a=2, t=2, d=32)
                        rs5 = rs[:, ti_, :].rearrange("p (a t d) -> p a t d", a=2, t=2)
                        for h in range(4):
                            for tt in range(2):
                                T.op('pool', lambda e, h=h, tt=tt, qn5=qn5, qr5=qr5, rs5=rs5: e.tensor_tensor(
                                    out=qr5[:, h, :, tt, :], in0=qn5[:, h, :, 1 - tt, :], in1=rs5[:, :, tt, :], op=ALU.mult),
                                    reads=[qnB[j], ropeB], writes=[qrB[j]])
                        T.op('dve', lambda e, qn4=qn4, cb_=cb_: e.tensor_tensor(out=qn4, in0=qn4, in1=cb_, op=ALU.mult),
                             reads=[ropeB], writes=[qnB[j]])
                        T.op('dve', lambda e, j=j: e.tensor_tensor(out=qb[j][:], in0=qn[j][:], in1=qr[j][:], op=ALU.add),
                             reads=[qnB[j], qrB[j]], writes=[qbB[j]])
                    else:
                        T.op('dve', lambda e, j=j: e.tensor_copy(out=qb[j][:], in_=qn[j][:]), reads=[qnB[j]], writes=[qbB[j]])
                    p_, pB_ = npb()

                    def f2(e, p_=p_, j=j):
                        for h in range(4):
                            ins = e.transpose(p_[:, h * 128:(h + 1) * 128], qb[j][:, h * 128:(h + 1) * 128], ident[:])
                        return ins
                    T.op('pe', f2, reads=[qbB[j], identB], writes=[pB_])
                    T.op('act', lambda e, p_=p_, j=j: e.activation(out=qtt[j][:], in_=p_[:, 0:512].rearrange(
                        "p (h t) -> p h t", t=128), func=AF.Copy), reads=[pB_], writes=[qttB[j]])
                    if kind == 'q':
                        T.dma('sp', qT[gidx * 4:(gidx + 1) * 4, :, t0:t0 + 128].rearrange("h d t -> d h t"), qtt[j][:],
                              reads=[qttB[j]], writes=[qTB])
                    else:
                        T.dma('sp', kT[:, :, t0:t0 + 128].rearrange("h d t -> d h t"), qtt[j][:],
                              reads=[qttB[j]], writes=[kTB])

            calls = []
            for blk in range(4):
                own = blk < 2
                calls.append(('h', None, lambda blk=blk: build_hT(xp[blk * 1024:(blk + 1) * 1024, :], 8, 0, 1)))
                for c4 in range(6):
                    calls.append(('w', c4 * 512, lambda c4=c4, blk=blk: chan_major4(
                        c4 * 512, 1024, lambda q: uT[c4 * 512 + q * 128:c4 * 512 + (q + 1) * 128, blk * 1024:(blk + 1) * 1024], uTB)))
                if own:
                    for c4 in range(8):
                        calls.append(('w', GA_OFF + c4 * 512, lambda c4=c4, blk=blk: chan_major4(
                            GA_OFF + c4 * 512, 1024, lambda q: gT[c4 * 512 + q * 128:c4 * 512 + (q + 1) * 128,
                                                                  blk * 1024:(blk + 1) * 1024], gTB, sig=True)))
                    for g in range(4):
                        calls.append(('w', Q_OFF + g * 512, lambda g=g, blk=blk: tok_major(Q_OFF + g * 512, 8, 'q', blk * 1024, gidx=g)))
                calls.append(('w', K_OFF, lambda blk=blk: tok_major(K_OFF, 8, 'k', blk * 1024)))
                calls.append(('w', V_OFF, lambda blk=blk: tok_major(V_OFF, 8, 'v', blk * 1024)))
            calls.append(('h', None, lambda: build_hT(ctx, 2, 2, 3)))
            calls.append(('w', K_OFF, lambda: tok_major(K_OFF, 2, 'k', L, rope=False)))
            calls.append(('w', V_OFF, lambda: tok_major(V_OFF, 2, 'v', L)))
            wcols = [c_[1] for c_ in calls if c_[0] == 'w']
            wi_ = 0
            preload(wcols[0])
            for kind_, col_, fn_ in calls:
                if kind_ == 'w':
                    wi_ += 1
                    if wi_ < len(wcols):
                        preload(wcols[wi_])
                fn_()
            if DEBUG:
                T.barrier()
                for nm, src, sB in (("d_uT", uT, uTB), ("d_qT", qT, qTB), ("d_kT", kT, kTB), ("d_v", vtok, vB), ("d_gT", gT, gTB)):
                    T.dma('sp', dbg[nm], src, reads=[sB], writes=[Buf("dbg_" + nm)])
            T.barrier()
        if STOP_AFTER <= 3:
            sH.close()
            return nc, T

        yaB = Buf("yaT", dram=True)
        with ExitStack() as s4:
            s4.enter_context(nc.named_scope("s4"))
            f1d = sb("f1d", [64, 130], BF16, s4); f1k = sb("f1k", [128, 130], BF16, s4)
            t3 = sb("t3", [128, 65, 4, 64], BF16, s4); e64 = sb("e64", [128, 128], BF16, s4)
            rt = sb("rt", [65, 64, 2, 32], BF16, s4); tabB = Buf("fft_tabs")
            for dst_, src_ in ((f1d, F1d), (f1k, F1k), (t3, T3), (e64, E64), (rt, Rt)):
                T.dma('sp', dst_[:], src_, writes=[tabB])
            ngt = sb("ngt", [128, 64], F32, s4); cw = sb("cw", [128, 24, 3], F32, s4); cbv = sb("cbv", [128, 24], F32, s4)
            fbs = sb("fbs", [128, 8], F32, s4); hal = sb("hal", [128, 2], F32, s4); smB = Buf("h_small")
            for dst_, src_ in ((ngt, negt), (cw, convw), (cbv, convb), (fbs, fbias), (hal, halo)):
                T.dma('sp', dst_[:], src_, writes=[smB])
            w4g = sb("w4g", [128, 128], BF16, s4); w4B = Buf("w4g")
            dlt = sb("dlt", [128, 128], F32, s4); dltB = Buf("dlt")
            dec = [sb("dec%d" % i, [128, 4, 128], F32, s4) for i in range(2)]; decB = [Buf("dec%d" % i) for i in range(2)]
            ksum = sb("ksum", [128, 128], F32, s4); ksumB = Buf("ksum")
            rnorm = sb("rnorm", [128, 128], F32, s4); rnB = Buf("rnorm")
            kfz = sb("kfz", [128, 128 * 64], BF16, s4); kfzB = Buf("kfz")
            kf = kfz[:].rearrange("p (c s) -> p c s", s=64)
            zF = kfz[0:64, :].rearrange("p (c s) -> p c s", s=64)
            ysb = kfz[0:32, :].rearrange("p (s c) -> p s c", c=128)
            act_ = sb("act_", [128, 2 * 65 * 128], BF16, s4); AB = Buf("A_CT"); CTB = AB
            A = act_[:, 0:65 * 128].rearrange("p (f c) -> p f c", f=65)
            CT = act_[0:65, 0:2 * 64 * 128].rearrange("p (r s c) -> p r s c", r=2, s=64)
            big3 = sb("big3", [128, 2 * 65 * 128], BF16, s4); KAB = Buf("KA")
            KA = big3[:, 0:8320].rearrange("p (f c) -> p f c", c=128)
            KB_ = big3[:, 8320:16640].rearrange("p (f c) -> p f c", c=128)
            ur = [big3[:, i * 4100:(i + 1) * 4100].rearrange("p (s t) -> p s t", s=2) for i in range(2)]
            urB = [Buf("ur%d" % i) for i in range(2)]
            cv_ = [big3[:, 8200 + i * L:8200 + (i + 1) * L] for i in range(2)]; cvB_ = [Buf("cvh%d" % i) for i in range(2)]
            x0c = sb("x0c", [128, HALF], BF16, s4); x0B = Buf("x0c")
            zbf = sb("zbf", [128, L], BF16, s4); zbB = Buf("zbf")
            tm = [sb("tm%d" % i, [128, 256], F32, s4) for i in range(2)]; tmB = [Buf("tm%d" % i) for i in range(2)]
            Y = sb("Y", [128, 128, 65], BF16, s4); YB = Buf("Y")
            yT = sb("yT", [128, HALF], F32, s4); yTB = Buf("yT")
            yab = sb("yab", [128, HALF], BF16, s4); yabB = Buf("yab")

            def step1(src, kk, f1t):
                for gi_, c0_ in enumerate(range(0, 128, 7)):
                    ncn = min(7, 128 - c0_)
                    ps, psB = npf()

                    def f(e, ps=ps, c0_=c0_, ncn=ncn):
                        for ci in range(ncn):
                            e.matmul(ps[0:64, ci * 65:(ci + 1) * 65], lhsT=src[0:kk, c0_ + ci, :], rhs=f1t[0:kk, 0:65],
                                     start=True, stop=True)
                            ins = e.matmul(ps[64:128, ci * 65:(ci + 1) * 65], lhsT=src[0:kk, c0_ + ci, :], rhs=f1t[0:kk, 65:130],
                                           start=True, stop=True)
                        return ins
                    T.op('pe', f, reads=[kfzB, tabB], writes=[psB])
                    src_ap = ps[:, 0:ncn * 65].rearrange("p (c f) -> p f c", f=65)
                    if gi_ % 2 == 0:
                        T.op('act', lambda e, src_ap=src_ap, c0_=c0_, ncn=ncn: e.activation(
                            out=A[:, :, c0_:c0_ + ncn], in_=src_ap, func=AF.Copy), reads=[psB], writes=[AB])
                    else:
                        T.op('dve', lambda e, src_ap=src_ap, c0_=c0_, ncn=ncn: e.tensor_copy(
                            out=A[:, :, c0_:c0_ + ncn], in_=src_ap), reads=[psB], writes=[AB])

            def step3_data(e, f1, ps):
                e.matmul(ps[:, 0:128], lhsT=t3[:, f1, 0:2, :].rearrange("p a m -> p (a m)"), rhs=A[:, f1, :], start=True, stop=True)
                return e.matmul(ps[:, 128:256], lhsT=t3[:, f1, 1:3, :].rearrange("p a m -> p (a m)"), rhs=A[:, f1, :],
                                start=True, stop=True)

            def step3_filt(e, f1, ps):
                ins = None
                for off, (bt, bb) in ((0, (0, 0)), (128, (3, 1))):
                    e.matmul(ps[0:64, off:off + 128], lhsT=t3[:, f1, bt, :], rhs=A[:, f1, :], start=True, stop=True)
                    ins = e.matmul(ps[64:128, off:off + 128], lhsT=t3[:, f1, bb, :], rhs=A[:, f1, :], start=True, stop=True)
                return ins

            for g in range(8):
                c0 = g * 128
                T.alias(urB + cvB_, [KAB])
                for which in (1, 2, 0):
                    ui = which % 2
                    u_, uB_ = ur[ui], urB[ui]
                    row0 = which * C + c0
                    T.dma('sp', u_[:, :, 1:HALF + 1], uT[row0:row0 + 128, :].rearrange("c (s t) -> c s t", s=2),
                          reads=[uTB], writes=[uB_])
                    for (seg, cell, sseg, scell, fl) in ((0, 0, 1, HALF, 0), (0, HALF + 1, 1, 1, 1), (1, 0, 0, HALF, 1),
                                                         (1, HALF + 1, 0, 1, 0)):
                        T.op('dve', lambda e, u_=u_, seg=seg, cell=cell, sseg=sseg, scell=scell, fl=fl: e.tensor_scalar(
                            out=u_[:, seg, cell:cell + 1], in0=u_[:, sseg, scell:scell + 1], scalar1=hal[:, fl:fl + 1],
                            scalar2=None, op0=ALU.mult), reads=[smB], writes=[uB_])
                    j = which * 8 + g
                    nseg = 1 if which == 0 else 2
                    if which == 0:
                        dstv, dstB_ = x0c, x0B
                    else:
                        dstv, dstB_ = cv_[which - 1], cvB_[which - 1]
                    for seg in range(nseg):
                        o = dstv[:, seg * HALF:(seg + 1) * HALF]
                        eng = 'dve'
                        T.op(eng, lambda e, o=o, u_=u_, seg=seg, j=j: e.tensor_scalar(
                            out=o, in0=u_[:, seg, 1:HALF + 1], scalar1=cw[:, j, 1:2], scalar2=cbv[:, j:j + 1],
                            op0=ALU.mult, op1=ALU.add), reads=[uB_, smB], writes=[dstB_])
                        T.op(eng, lambda e, o=o, u_=u_, seg=seg, j=j: e.scalar_tensor_tensor(
                            out=o, in0=u_[:, seg, 0:HALF], scalar=cw[:, j, 0:1], in1=o, op0=ALU.mult, op1=ALU.add),
                            reads=[uB_, smB], writes=[dstB_])
                        T.op(eng, lambda e, o=o, u_=u_, seg=seg, j=j: e.scalar_tensor_tensor(
                            out=o, in0=u_[:, seg, 2:HALF + 2], scalar=cw[:, j, 2:3], in1=o, op0=ALU.mult, op1=ALU.add),
                            reads=[uB_, smB], writes=[dstB_])
                T.op('dve', lambda e: e.tensor_tensor(out=zbf[:, 0:HALF], in0=cv_[0][:, 0:HALF], in1=cv_[1][:, 0:HALF], op=ALU.mult),
                     reads=[cvB_[0], cvB_[1]], writes=[zbB])
                T.op('pool', lambda e: e.tensor_tensor(out=zbf[:, HALF:L], in0=cv_[0][:, HALF:L], in1=cv_[1][:, HALF:L],
                                                       op=ALU.mult), reads=[cvB_[0], cvB_[1]], writes=[zbB])
                T.alias([KAB], urB + cvB_)
                T.dma('pool', w4g[:], fw4[:, c0:c0 + 128], writes=[w4B])
                T.dma('sp', dlt[:], delta[:, c0:c0 + 128], writes=[dltB])
                hid3 = hid[:].rearrange("p (a b) -> p b a", b=64)
                for q4 in range(16):
                    ps, psB = npf()

                    def f(e, ps=ps, q4=q4):
                        for j in range(4):
                            ins = e.matmul(ps[:, j * 128:(j + 1) * 128], lhsT=hid3[:, q4 * 4 + j, :], rhs=w4g[:],
                                           start=True, stop=True)
                        return ins
                    T.op('pe', f, reads=[hidB, w4B], writes=[psB])
                    di = q4 % 2
                    for j in range(4):
                        s2 = q4 * 4 + j
                        T.op('act', lambda e, di=di, j=j, s2=s2: e.activation(out=dec[di][:, j, :], in_=dlt[:], func=AF.Exp,
                                                                               scale=ngt[:, s2:s2 + 1]),
                             reads=[dltB, smB], writes=[decB[di]])
                    T.op('dve', lambda e, ps=ps, di=di, q4=q4: e.tensor_tensor(
                        out=kf[:, :, q4 * 4:(q4 + 1) * 4].rearrange("p c s -> p s c"),
                        in0=ps[:].rearrange("p (s c) -> p s c", c=128), in1=dec[di][:], op=ALU.mult),
                        reads=[psB, decB[di]], writes=[kfzB])
                T.op('dve', lambda e: e.tensor_reduce(out=ksum[:], in_=kf, axis=AX.X, op=ALU.add, apply_absolute_value=True),
                     reads=[kfzB], writes=[ksumB])
                ps, psB = npf()
                T.op('pe', lambda e, ps=ps: e.matmul(ps[:, 0:128], lhsT=ones_f[:], rhs=ksum[:], start=True, stop=True),
                     reads=[ksumB, onesB], writes=[psB])
                T.op('dve', lambda e, ps=ps: e.reciprocal(out=rnorm[:], in_=ps[:, 0:128]), reads=[psB], writes=[rnB])
                step1(kf, 128, f1k)
                for f1 in range(65):
                    ps, psB = npf()

                    def f(e, ps=ps, f1=f1):
                        return step3_filt(e, f1, ps)
                    T.op('pe', f, reads=[AB, tabB], writes=[psB])
                    T.op('dve', lambda e, ps=ps, f1=f1: e.tensor_tensor(out=KA[:, f1, :], in0=ps[:, 0:128], in1=rnorm[:],
                                                                        op=ALU.mult), reads=[psB, rnB], writes=[KAB])
                    T.op('dve', lambda e, ps=ps, f1=f1: e.tensor_tensor(out=KB_[:, f1, :], in0=ps[:, 128:256], in1=rnorm[:],
                                                                        op=ALU.mult), reads=[psB, rnB], writes=[KAB])
                zv = zbf[:].rearrange("p (a b) -> p b a", b=64)
                for q8 in range(8):
                    p_, pB_ = npb()

                    def f(e, p_=p_, q8=q8):
                        for j in range(8):
                            ins = e.transpose(p_[0:64, j * 128:(j + 1) * 128], zv[:, q8 * 8 + j, :], ident[:])
                        return ins
                    T.op('pe', f, reads=[zbB, identB], writes=[pB_])
                    T.op('act', lambda e, p_=p_, q8=q8: e.activation(
                        out=zF[:, :, q8 * 8:(q8 + 1) * 8].rearrange("p c s -> p s c"),
                        in_=p_[0:64, :].rearrange("p (s c) -> p s c", c=128), func=AF.Copy), reads=[pB_], writes=[kfzB])
                step1(zF, 64, f1d)
                for f1 in range(65):
                    ps, psB = npf()

                    def f(e, ps=ps, f1=f1):
                        return step3_data(e, f1, ps)
                    T.op('pe', f, reads=[AB, tabB], writes=[psB])
                    ti_ = f1 % 2
                    T.op('dve', lambda e, ps=ps, f1=f1, ti_=ti_: e.tensor_tensor(out=tm[ti_][:, 0:128], in0=ps[:, 0:128],
                                                                                 in1=KA[:, f1, :], op=ALU.mult),
                         reads=[psB, KAB], writes=[tmB[ti_]])
                    T.op('dve', lambda e, ps=ps, f1=f1, ti_=ti_: e.tensor_tensor(out=tm[ti_][:, 128:256], in0=ps[:, 128:256],
                                                                                 in1=KB_[:, f1, :], op=ALU.mult),
                         reads=[psB, KAB], writes=[tmB[ti_]])
                    T.op('pool', lambda e, f1=f1, ti_=ti_: e.tensor_tensor(out=Y[:, :, f1], in0=tm[ti_][:, 0:128],
                                                                           in1=tm[ti_][:, 128:256], op=ALU.add),
                         reads=[tmB[ti_]], writes=[YB])
                for c4 in range(32):
                    ps, psB = npf()

                    def f(e, ps=ps, c4=c4):
                        for ci in range(4):
                            ins = e.matmul(ps[0:65, ci * 128:(ci + 1) * 128], lhsT=Y[:, c4 * 4 + ci, :], rhs=e64[:],
                                           start=True, stop=True)
                        return ins
                    T.op('pe', f, reads=[YB, tabB], writes=[psB])
                    src_ap = ps[0:65, :].rearrange("p (c r s) -> p r s c", r=2, s=64)
                    if c4 % 2 == 0:
                        T.op('act', lambda e, src_ap=src_ap, c4=c4: e.activation(out=CT[:, :, :, c4 * 4:(c4 + 1) * 4], in_=src_ap,
                                                                                 func=AF.Copy), reads=[psB], writes=[CTB])
                    else:
                        T.op('dve', lambda e, src_ap=src_ap, c4=c4: e.tensor_copy(out=CT[:, :, :, c4 * 4:(c4 + 1) * 4], in_=src_ap),
                             reads=[psB], writes=[CTB])
                for q4 in range(16):
                    ps, psB = npf()

                    def f(e, ps=ps, q4=q4):
                        for j in range(4):
                            s2 = q4 * 4 + j
                            e.matmul(ps[0:32, j * 128:(j + 1) * 128], lhsT=rt[:, s2, 0, :], rhs=CT[:, 0, s2, :], start=True, stop=False)
                            ins = e.matmul(ps[0:32, j * 128:(j + 1) * 128], lhsT=rt[:, s2, 1, :], rhs=CT[:, 1, s2, :],
                                           start=False, stop=True)
                        return ins
                    T.op('pe', f, reads=[CTB, tabB], writes=[psB])
                    T.op('act', lambda e, ps=ps, q4=q4: e.activation(out=ysb[:, q4 * 4:(q4 + 1) * 4, :],
                                                                     in_=ps[0:32, :].rearrange("p (s c) -> p s c", c=128),
                                                                     func=AF.Copy), reads=[psB], writes=[kfzB])
                yT3 = yT[:].rearrange("p (a b) -> p b a", b=64)
                for q2 in range(2):
                    p_, pB_ = npb()

                    def f(e, p_=p_, q2=q2):
                        for j in range(32):
                            ins = e.transpose(p_[:, j * 32:(j + 1) * 32], ysb[:, q2 * 32 + j, :], ident[0:32, 0:32])
                        return ins
                    T.op('pe', f, reads=[kfzB, identB], writes=[pB_])
                    T.op('dve', lambda e, p_=p_, q2=q2: e.tensor_copy(out=yT3[:, q2 * 32:(q2 + 1) * 32, :],
                                                                      in_=p_[:, :].rearrange("p (s a) -> p s a", a=32)),
                         reads=[pB_], writes=[yTB])
                T.op('dve', lambda e, g=g: e.scalar_tensor_tensor(out=yT[:], in0=zbf[:, 0:HALF], scalar=fbs[:, g:g + 1], in1=yT[:],
                                                                  op0=ALU.mult, op1=ALU.add), reads=[zbB, smB], writes=[yTB])
                T.op('pool', lambda e: e.tensor_tensor(out=yab[:], in0=yT[:], in1=x0c[:], op=ALU.mult),
                     reads=[yTB, x0B], writes=[yabB])
                T.dma('sp', yaT[c0:c0 + 128, :], yab[:], reads=[yabB], writes=[yaB])
            if DEBUG:
                T.barrier()
                T.dma('sp', dbg["d_ya"], yaT, reads=[yaB], writes=[Buf("dbg_ya")])
            T.barrier()
        sH.close()
        if STOP_AFTER <= 4:
            return nc, T

        ybB = Buf("ybT", dram=True)
        with ExitStack() as s5:
            s5.enter_context(nc.named_scope("s5"))
            kt = [sb("kt%d" % i, [128, NKEY], BF16, s5) for i in range(2)]; ktB = [Buf("kt%d" % i) for i in range(2)]
            vt = [sb("vt%d" % i, [128, 34, 128], BF16, s5) for i in range(2)]; vtB = [Buf("vtt%d" % i) for i in range(2)]
            qt = [sb("qt%d" % i, [128, 512], BF16, s5) for i in range(2)]; qtB = [Buf("qt%d" % i) for i in range(2)]
            pt = [sb("pt%d" % i, [128, 512], BF16, s5) for i in range(4)]; ptB = [Buf("pt%d" % i) for i in range(4)]
            acc = [sb("acc%d" % i, [128, 512], F32, s5) for i in range(4)]; accB = [Buf("acc%d" % i) for i in range(4)]
            rsm = sb("rsm", [128, 512], F32, s5); rsmB = Buf("rsm")
            ob = [sb("ob%d" % i, [128, 512], BF16, s5) for i in range(2)]; obB = [Buf("ob%d" % i) for i in range(2)]
            sc_i = [0]
            qi = 0; pi = 0
            for g in range(4):
                gi = g % 2
                T.dma('sp', kt[gi][:], kT[g], reads=[kTB], writes=[ktB[gi]])
                T.dma('sp', vt[gi][:], vtok[:, g * 128:(g + 1) * 128].rearrange("(j p) d -> p j d", p=128),
                      reads=[vB], writes=[vtB[gi]])
                for hh in range(4):
                    h = g * 4 + hh
                    for qb_ in range(4):
                        qq = qi % 2; qi += 1
                        T.dma('sp', qt[qq][:], qT[h, :, qb_ * 512:(qb_ + 1) * 512], reads=[qTB], writes=[qtB[qq]])
                        po_, poB = pf[3], pfB[3]
                        psm, psmB = pf[4], pfB[4]

                        def score(j, gi=gi, qq=qq):
                            si = sc_i[0] % 3; sc_i[0] += 1
                            ps, psB = pf[si], pfB[si]
                            T.op('pe', lambda e, ps=ps, j=j: e.matmul(
                                ps[:], lhsT=kt[gi][:, j * 128:(j + 1) * 128], rhs=qt[qq][:], start=True, stop=True),
                                reads=[ktB[gi], qtB[qq]], writes=[psB])
                            return ps, psB
                        nxt = score(0)
                        for j in range(34):
                            ps, psB = nxt
                            pp = pi % 4; pi += 1
                            T.op('act', lambda e, ps=ps, pp=pp: e.activation(out=pt[pp][:], in_=ps[:], func=AF.Exp,
                                                                             scale=128.0 ** -0.5),
                                 reads=[psB], writes=[ptB[pp]])
                            if j + 1 < 34:
                                nxt = score(j + 1)

                            T.op('pe', lambda e, pp=pp, gi=gi, j=j, po_=po_: e.matmul(
                                po_[:], lhsT=vt[gi][:, j, :], rhs=pt[pp][:], start=(j == 0), stop=(j == 33)),
                                reads=[ptB[pp], vtB[gi]], writes=[poB])
                            r3 = j % 3
                            if r3 == 2:
                                T.op('pe', lambda e, pp=pp, j=j, psm=psm: e.matmul(
                                    psm[:], lhsT=ones_b[:], rhs=pt[pp][:], start=(j == 2), stop=False),
                                    reads=[ptB[pp], onesbB], writes=[psmB])
                            else:
                                ai = r3 * 2 + ((j // 3) % 2)
                                aeng = 'dve' if r3 == 0 else 'pool'
                                if j // 3 < 2:
                                    T.op(aeng, lambda e, pp=pp, ai=ai: e.tensor_copy(out=acc[ai][:], in_=pt[pp][:]),
                                         reads=[ptB[pp]], writes=[accB[ai]])
                                else:
                                    T.op(aeng, lambda e, pp=pp, ai=ai: e.tensor_tensor(out=acc[ai][:], in0=acc[ai][:], in1=pt[pp][:],
                                                                                       op=ALU.add), reads=[ptB[pp]], writes=[accB[ai]])

                        def fs(e, psm=psm):
                            for ai in range(4):
                                ins = e.matmul(psm[:], lhsT=ones_f[:], rhs=acc[ai][:], start=False, stop=(ai == 3))
                            return ins
                        T.op('pe', fs, reads=accB + [onesB], writes=[psmB])
                        T.op('dve', lambda e, psm=psm: e.reciprocal(out=rsm[:], in_=psm[:]), reads=[psmB], writes=[rsmB])
                        oo = qi % 2
                        T.op('dve', lambda e, po_=po_, oo=oo: e.tensor_tensor(out=ob[oo][:], in0=po_[:], in1=rsm[:], op=ALU.mult),
                             reads=[poB, rsmB], writes=[obB[oo]])
                        T.dma('sp', ybT[h * 128:(h + 1) * 128, qb_ * 512:(qb_ + 1) * 512], ob[oo][:], reads=[obB[oo]],
                              writes=[ybB])
            if DEBUG:
                T.barrier()
                T.dma('sp', dbg["d_yb"], ybT, reads=[ybB], writes=[Buf("dbg_yb")])
            T.barrier()
        if STOP_AFTER <= 5:
            return nc, T

        x1B = Buf("x1s", dram=True)
        with ExitStack() as s6:
            s6.enter_context(nc.named_scope("s6"))
            ya_s = sb("ya_s", [128, 8, 512], BF16, s6); yaSB = Buf("ya_s")
            yb_s = sb("yb_s", [128, 16, 512], BF16, s6); ybSB = Buf("yb_s")
            mg = sb("mg", [128, 16, 512], BF16, s6); mgB = Buf("mg")
            wa_ = [sb("wm_a%d" % i, [128, 8, 512], BF16, s6) for i in range(2)]; waB_ = [Buf("wm_a%d" % i) for i in range(2)]
            wb_ = [sb("wm_b%d" % i, [128, 16, 512], BF16, s6) for i in range(2)]; wbB_ = [Buf("wm_b%d" % i) for i in range(2)]
            gg = [sb("gg%d" % i, [128, 2, 512], BF16, s6) for i in range(2)]; ggB = [Buf("gg%d" % i) for i in range(2)]
            t1 = [sb("t1_%d" % i, [128, 512], F32, s6) for i in range(2)]; t1B = [Buf("t1_%d" % i) for i in range(2)]
            wo_ = [sb("wo%d" % i, [128, 16, 512], BF16, s6) for i in range(2)]; woB = [Buf("wo%d" % i) for i in range(2)]
            mix = [sb("mix%d" % i, [128, D], F32, s6) for i in range(4)]; mixB = [Buf("mix%d" % i) for i in range(4)]
            xo = [sb("xo%d" % i, [128, D], F32, s6) for i in range(2)]; xoB = [Buf("xo%d" % i) for i in range(2)]
            jk = sb("jk6", [128, D], BF16, s6); jkB = Buf("jk6")
            st6 = [sb("st6_%d" % i, [128, 4], F32, s6) for i in range(2)]; st6B = [Buf("st6_%d" % i) for i in range(2)]
            wi = 0
            for tb in range(4):
                ts_ = slice(tb * 512, (tb + 1) * 512)
                T.dma('sp', ya_s[:], yaT[:, ts_].rearrange("(k p) t -> p k t", p=128), reads=[yaB], writes=[yaSB])
                T.dma('sp', yb_s[:], ybT[:, ts_].rearrange("(k p) t -> p k t", p=128), reads=[ybB], writes=[ybSB])
                for oc4 in range(4):
                    i = wi % 2; wi += 1
                    T.dma('sp', wa_[i][:], wbab[:, oc4 * 512:(oc4 + 1) * 512].rearrange("(k p) c -> p k c", p=128), reads=[wmB], writes=[waB_[i]])
                    T.dma('sp', wb_[i][:], wbbb[:, oc4 * 512:(oc4 + 1) * 512].rearrange("(k p) c -> p k c", p=128), reads=[wmB], writes=[wbB_[i]])
                    for q in range(4):
                        oc = oc4 * 4 + q
                        gi_ = oc % 2
                        T.dma('sp', gg[gi_][:, 0, :], gT[oc * 128:(oc + 1) * 128, ts_], reads=[gTB], writes=[ggB[gi_]])
                        T.dma('sp', gg[gi_][:, 1, :], gT[D + oc * 128:D + (oc + 1) * 128, ts_], reads=[gTB], writes=[ggB[gi_]])
                        pa, paB = npf()

                        def fa(e, pa=pa, i=i, q=q):
                            for k in range(8):
                                ins = e.matmul(pa[:], lhsT=wa_[i][:, k, q * 128:(q + 1) * 128], rhs=ya_s[:, k, :], start=(k == 0), stop=(k == 7))
                            return ins
                        T.op('pe', fa, reads=[waB_[i], yaSB], writes=[paB])
                        pb2, pb2B = npf()

                        def fb(e, pb2=pb2, i=i, q=q):
                            for k in range(16):
                                ins = e.matmul(pb2[:], lhsT=wb_[i][:, k, q * 128:(q + 1) * 128], rhs=yb_s[:, k, :], start=(k == 0), stop=(k == 15))
                            return ins
                        T.op('pe', fb, reads=[wbB_[i], ybSB], writes=[pb2B])
                        T.op('dve', lambda e, pa=pa, gi_=gi_: e.tensor_tensor(out=t1[gi_][:], in0=pa[:], in1=gg[gi_][:, 0, :], op=ALU.mult),
                             reads=[paB, ggB[gi_]], writes=[t1B[gi_]])
                        T.op('dve', lambda e, pb2=pb2, gi_=gi_: e.tensor_tensor(out=gg[gi_][:, 1, :], in0=pb2[:], in1=gg[gi_][:, 1, :], op=ALU.mult),
                             reads=[pb2B], writes=[ggB[gi_]])
                        T.op('pool', lambda e, gi_=gi_, oc=oc: e.tensor_tensor(out=mg[:, oc, :], in0=t1[gi_][:], in1=gg[gi_][:, 1, :], op=ALU.add),
                             reads=[t1B[gi_], ggB[gi_]], writes=[mgB])
                for cb4 in range(4):
                    i = wi % 2; wi += 1
                    T.dma('sp', wo_[i][:], wob[:, cb4 * 512:(cb4 + 1) * 512].rearrange("(k p) c -> p k c", p=128), reads=[wmB], writes=[woB[i]])
                    for tl in range(4):
                        ps, psB = npf()

                        def fo(e, ps=ps, i=i, tl=tl):
                            for k in range(16):
                                ins = e.matmul(ps[:], lhsT=mg[:, k, tl * 128:(tl + 1) * 128], rhs=wo_[i][:, k, :],
                                               start=(k == 0), stop=(k == 15))
                            return ins
                        T.op('pe', fo, reads=[woB[i], mgB], writes=[psB])
                        T.op('act', lambda e, ps=ps, tl=tl, cb4=cb4: e.activation(out=mix[tl][:, cb4 * 512:(cb4 + 1) * 512], in_=ps[:],
                                                                                  func=AF.Copy), reads=[psB], writes=[mixB[tl]])
                for tl in range(4):
                    i = tl % 2
                    r0 = tb * 512 + tl * 128
                    T.dma('sp', xo[i][:], xp[r0:r0 + 128, :], writes=[xoB[i]])
                    T.op('act', lambda e, tl=tl, i=i: e.activation(out=jk[:], in_=mix[tl][:], func=AF.Square, accum_out=st6[i][:, 0:1]),
                         reads=[mixB[tl]], writes=[jkB, st6B[i]])
                    T.op('act', lambda e, i=i: e.activation(out=st6[i][:, 1:2], in_=st6[i][:, 0:1], func=AF.Sqrt, bias=epsT[:, 0:1],
                                                            scale=1.0 / D), reads=[epsB], writes=[st6B[i]])
                    T.op('dve', lambda e, i=i: e.reciprocal(out=st6[i][:, 2:3], in_=st6[i][:, 1:2]), writes=[st6B[i]])
                    T.op('dve', lambda e, tl=tl, i=i: e.scalar_tensor_tensor(out=mix[tl][:], in0=mix[tl][:], scalar=st6[i][:, 2:3],
                                                                             in1=Grow[:, 0, :], op0=ALU.mult, op1=ALU.mult),
                         reads=[st6B[i], GrowB], writes=[mixB[tl]])
                    T.op('pool', lambda e, tl=tl, i=i: e.tensor_tensor(out=xo[i][:], in0=xo[i][:], in1=mix[tl][:], op=ALU.add),
                         reads=[mixB[tl]], writes=[xoB[i]])
                    T.dma('sp', x1s[r0:r0 + 128, :], xo[i][:], reads=[xoB[i]], writes=[x1B])
            if DEBUG:
                T.barrier()
                T.dma('sp', dbg["d_x1"], x1s, reads=[x1B], writes=[Buf("dbg_x1")])
            T.barrier()
        if STOP_AFTER <= 6:
            return nc, T

        outB = Buf("out", dram=True)
        with ExitStack() as s7:
            s7.enter_context(nc.named_scope("s7"))
            xt7 = [sb("x7_%d" % i, [128, D], F32, s7) for i in range(2)]; xt7B = [Buf("x7_%d" % i) for i in range(2)]
            xn7 = [sb("xn7_%d" % i, [128, D], BF16, s7) for i in range(2)]; xn7B = [Buf("xn7_%d" % i) for i in range(2)]
            jk7 = sb("jk7", [128, D], BF16, s7); jk7B = Buf("jk7")
            st7 = [sb("st7_%d" % i, [128, 4], F32, s7) for i in range(4)]; st7B = [Buf("st7_%d" % i) for i in range(4)]
            h2T = sb("h2T", [128, 16, 512], BF16, s7); h2B_ = Buf("h2T")
            uT7 = sb("uT7", [128, 64, 512], BF16, s7); u7B = Buf("uT7")
            w1s = [sb("w1s%d" % i, [128, 16, 128], BF16, s7) for i in range(3)]; w1sB = [Buf("w1s%d" % i) for i in range(3)]
            w2s = [sb("w2s%d" % i, [128, 16, 512], BF16, s7) for i in range(2)]; w2sB = [Buf("w2s%d" % i) for i in range(2)]
            fo7 = [sb("fo7_%d" % i, [128, D], F32, s7) for i in range(4)]; fo7B = [Buf("fo7_%d" % i) for i in range(4)]
            rl7 = [sb("rl7_%d" % i, [128, 512], F32, s7) for i in range(2)]; rl7B = [Buf("rl7_%d" % i) for i in range(2)]
            w1i = 0; w2i = 0
            for tb in range(4):
                for tl in range(4):
                    r0 = tb * 512 + tl * 128
                    i2 = tl % 2
                    T.dma('sp', xt7[i2][:], x1s[r0:r0 + 128, :], reads=[x1B], writes=[xt7B[i2]])
                    T.op('act', lambda e, tl=tl, i2=i2: e.activation(out=jk7[:], in_=xt7[i2][:], func=AF.Square, accum_out=st7[tl][:, 0:1]),
                         reads=[xt7B[i2]], writes=[jk7B, st7B[tl]])
                    T.op('act', lambda e, tl=tl: e.activation(out=st7[tl][:, 1:2], in_=st7[tl][:, 0:1], func=AF.Sqrt, bias=epsT[:, 0:1],
                                                              scale=1.0 / D), reads=[epsB], writes=[st7B[tl]])
                    T.op('dve', lambda e, tl=tl: e.reciprocal(out=st7[tl][:, 2:3], in_=st7[tl][:, 1:2]), writes=[st7B[tl]])
                    T.op('dve', lambda e, tl=tl, i2=i2: e.tensor_scalar(out=xn7[i2][:], in0=xt7[i2][:], scalar1=st7[tl][:, 2:3],
                                                                        scalar2=None, op0=ALU.mult),
                         reads=[xt7B[i2], st7B[tl]], writes=[xn7B[i2]])
                    for hb in range(2):
                        p_, pB_ = npb()

                        def f(e, p_=p_, i2=i2, hb=hb):
                            for j in range(8):
                                k = hb * 8 + j
                                ins = e.transpose(p_[:, j * 128:(j + 1) * 128], xn7[i2][:, k * 128:(k + 1) * 128], ident[:])
                            return ins
                        T.op('pe', f, reads=[xn7B[i2], identB], writes=[pB_])
                        for j in range(8):
                            k = hb * 8 + j
                            T.op('act', lambda e, p_=p_, j=j, k=k, tl=tl: e.activation(
                                out=h2T[:, k, tl * 128:(tl + 1) * 128], in_=p_[:, j * 128:(j + 1) * 128], func=AF.Identity,
                                bias=modv[:, 5, k:k + 1], scale=modv[:, 4, k:k + 1]), reads=[pB_, modB], writes=[h2B_])
                for fc in range(64):
                    i = w1i % 3; w1i += 1
                    T.dma('sp', w1s[i][:], wf1b[:, fc * 128:(fc + 1) * 128].rearrange("(k p) c -> p k c", p=128),
                          reads=[wf1B], writes=[w1sB[i]])
                    ps, psB = npf()

                    def f(e, ps=ps, i=i):
                        for k in range(16):
                            ins = e.matmul(ps[:], lhsT=w1s[i][:, k, :], rhs=h2T[:, k, :], start=(k == 0), stop=(k == 15))
                        return ins
                    T.op('pe', f, reads=[w1sB[i], h2B_], writes=[psB])
                    ri_ = fc % 2
                    T.op('act', lambda e, ps=ps, ri_=ri_: e.activation(out=rl7[ri_][:], in_=ps[:], func=AF.Relu),
                         reads=[psB], writes=[rl7B[ri_]])
                    T.op('dve', lambda e, fc=fc, ri_=ri_: e.tensor_tensor(out=uT7[:, fc, :], in0=rl7[ri_][:], in1=rl7[ri_][:],
                                                                          op=ALU.mult), reads=[rl7B[ri_]], writes=[u7B])
                for cb4 in range(4):
                    accs = [npf() for _ in range(4)]
                    for kp in range(4):
                        i = w2i % 2; w2i += 1
                        T.dma('sp', w2s[i][:], wf2b[kp * 2048:(kp + 1) * 2048, cb4 * 512:(cb4 + 1) * 512].rearrange(
                            "(k p) c -> p k c", p=128), reads=[wf2B], writes=[w2sB[i]])
                        for tl in range(4):
                            ps, psB = accs[tl]

                            def f(e, ps=ps, i=i, tl=tl, kp=kp):
                                for k in range(16):
                                    ins = e.matmul(ps[:], lhsT=uT7[:, kp * 16 + k, tl * 128:(tl + 1) * 128], rhs=w2s[i][:, k, :],
                                                   start=(kp == 0 and k == 0), stop=(kp == 3 and k == 15))
                                return ins
                            T.op('pe', f, reads=[w2sB[i], u7B], writes=[psB])
                    for tl in range(4):
                        ps, psB = accs[tl]
                        T.op('act', lambda e, ps=ps, tl=tl, cb4=cb4: e.activation(out=fo7[tl][:, cb4 * 512:(cb4 + 1) * 512], in_=ps[:],
                                                                                  func=AF.Copy), reads=[psB], writes=[fo7B[tl]])
                for tl in range(4):
                    r0 = tb * 512 + tl * 128
                    i2 = tl % 2
                    T.dma('sp', xt7[i2][:], x1s[r0:r0 + 128, :], reads=[x1B], writes=[xt7B[i2]])
                    T.op('act', lambda e, tl=tl: e.activation(out=jk7[:], in_=fo7[tl][:], func=AF.Square, accum_out=st7[tl][:, 0:1]),
                         reads=[fo7B[tl]], writes=[jk7B, st7B[tl]])
                    T.op('act', lambda e, tl=tl: e.activation(out=st7[tl][:, 1:2], in_=st7[tl][:, 0:1], func=AF.Sqrt, bias=epsT[:, 0:1],
                                                              scale=1.0 / D), reads=[epsB], writes=[st7B[tl]])
                    T.op('dve', lambda e, tl=tl: e.reciprocal(out=st7[tl][:, 2:3], in_=st7[tl][:, 1:2]), writes=[st7B[tl]])
                    T.op('dve', lambda e, tl=tl: e.scalar_tensor_tensor(out=fo7[tl][:], in0=fo7[tl][:], scalar=st7[tl][:, 2:3],
                                                                        in1=Grow[:, 1, :], op0=ALU.mult, op1=ALU.mult),
                         reads=[st7B[tl], GrowB], writes=[fo7B[tl]])
                    T.op('pool', lambda e, tl=tl, i2=i2: e.tensor_tensor(out=fo7[tl][:], in0=fo7[tl][:], in1=xt7[i2][:], op=ALU.add),
                         reads=[xt7B[i2]], writes=[fo7B[tl]])
                    T.dma('sp', out[r0:r0 + 128, :], fo7[tl][:], reads=[fo7B[tl]], writes=[outB])
            T.barrier()
        return nc, T


def host_consts():
    bf = ml_dtypes.bfloat16
    t = {}
    slot = np.arange(2 * L)
    d = np.where(slot < L, slot, 2 * L - slot).astype(np.float64)
    tt = d / (L - 1)
    wpos = 2 * np.pi * d / L
    bands = np.linspace(1e-4, 7, 8)
    emb = np.concatenate([tt[:, None], np.cos(bands * wpos[:, None]), -np.sin(bands * wpos[:, None])], axis=1)
    t["embT"] = np.ascontiguousarray(emb.T).astype(np.float32)
    t["negt"] = np.ascontiguousarray((-tt).reshape(128, 64)).astype(np.float32)
    MIN_DECAY = math.log(1e-2) / 1.5; MAX_DECAY = math.log(1e-2) / 0.3
    dl = np.abs(np.linspace(MIN_DECAY, MAX_DECAY, C, dtype=np.float32))
    t["delta"] = np.ascontiguousarray(np.broadcast_to(dl, (128, C))).astype(np.float32)
    s1 = np.arange(128)[:, None]; f1 = np.arange(65)[None, :]
    th = 2 * np.pi * s1 * f1 / 128
    t["F1k"] = np.concatenate([np.cos(th), -np.sin(th)], axis=1).astype(bf)
    s2 = np.arange(64)[:, None, None]; ff1 = np.arange(65)[None, :, None]; f2 = np.arange(64)[None, None, :]
    th = 2 * np.pi * s2 * (ff1 + 128 * f2) / 8192.0
    Mr, Mi = np.cos(th), -np.sin(th)
    P_ = np.concatenate([Mr, -Mi], axis=0); Q_ = np.concatenate([Mi, Mr], axis=0)
    t["T3"] = np.stack([P_, Q_, P_, -Q_], axis=2).astype(bf)
    f2 = np.arange(64)[:, None]; s2 = np.arange(64)[None, :]
    th = 2 * np.pi * f2 * s2 / 64
    E = np.zeros((2, 64, 2, 64))
    E[0, :, 0, :] = np.cos(th); E[1, :, 0, :] = -np.sin(th); E[0, :, 1, :] = np.sin(th); E[1, :, 1, :] = np.cos(th)
    t["E64"] = E.reshape(128, 128).astype(bf)
    t["ident"] = np.eye(128).astype(bf)
    return t


def host_tables(hf, consts):
    bf = ml_dtypes.bfloat16
    t = {}
    pos = (np.arange(L) + hf * HALF) % L
    rows = (pos // 64).astype(np.float32); cols = (pos % 64).astype(np.float32)
    inv = (10000.0 ** (-np.arange(0, 64, 2, dtype=np.float32) / 64)).astype(np.float32)
    ang = np.stack([rows[:, None] * inv, cols[:, None] * inv], axis=1)
    c_, s_ = np.cos(ang), np.sin(ang)
    C128 = np.concatenate([c_[:, 0], c_[:, 0], c_[:, 1], c_[:, 1]], axis=1)
    S128 = np.concatenate([-s_[:, 0], s_[:, 0], -s_[:, 1], s_[:, 1]], axis=1)
    t["ropec"] = np.ascontiguousarray(C128.reshape(32, 128, 128).transpose(1, 0, 2)).astype(bf)
    t["ropes"] = np.ascontiguousarray(S128.reshape(32, 128, 128).transpose(1, 0, 2)).astype(bf)
    s1t = (np.arange(64) + 32 * hf) % 64
    t["F1d"] = np.ascontiguousarray(consts["F1k"][s1t, :])
    f1 = np.arange(65)[:, None, None]; s2 = np.arange(64)[None, :, None]; s1 = (np.arange(32) + 32 * hf)[None, None, :]
    ph = 2 * np.pi * f1 * (64 * s1 + s2) / 8192.0
    w = np.where((f1 == 0) | (f1 == 64), 1.0, 2.0) / 8192.0
    t["Rt"] = np.stack([w * np.cos(ph), -w * np.sin(ph)], axis=2).astype(bf)
    t["halo"] = np.ascontiguousarray(np.broadcast_to(np.array([hf, 1 - hf], np.float32), (128, 2)))
    return t


_NC = [None]


def make_in_maps(inputs):
    inp = {k: np.asarray(v) for k, v in inputs.items()}
    consts = host_consts()
    f32 = np.float32
    col = lambda v, k: np.ascontiguousarray(np.asarray(v, f32).reshape(k, 128).T)
    shared = dict(consts)
    shared["w_ada"] = inp["w_ada"][0]; shared["w_in"] = inp["w_in"][0]
    shared["bada"] = col(inp["b_ada"][0], 96)
    ng = inp["norm_gains"][0]; ba = inp["b_ada"][0]
    rowb = np.stack([ba[2 * D:3 * D], ba[5 * D:6 * D], ng[1], ng[3]], axis=0)
    shared["rowb"] = np.ascontiguousarray(np.broadcast_to(rowb[None], (128, 4, D))).astype(f32)
    shared["gpre"] = np.ascontiguousarray(np.stack([col(ng[0], 16), col(ng[2], 16)], axis=1))
    cw = inp["conv_w"][0]
    shared["convw"] = np.ascontiguousarray(np.stack([col(cw[j], 24) for j in range(3)], axis=2))
    shared["convb"] = col(inp["conv_b"][0], 24)
    shared["fw1"] = inp["filt_w1"][0]; shared["fw2"] = inp["filt_w2"][0]
    shared["fw3d"] = np.ascontiguousarray(np.concatenate([inp["filt_w3"][0]] * 2, axis=1))
    fv = np.stack([inp["filt_b1"][0], inp["filt_b2"][0], inp["filt_b3"][0], inp["filt_freq"][0]], axis=1)
    shared["fvec"] = np.ascontiguousarray(np.concatenate([fv, fv], axis=0)).astype(f32)
    w4 = inp["filt_w4"][0]
    shared["fw4"] = np.ascontiguousarray(np.concatenate([w4[:, :C], w4[:, C:]], axis=0))
    shared["fbias"] = col(inp["filt_bias"][0], 8)
    shared["qkg"] = np.ascontiguousarray(np.broadcast_to(inp["qk_gains"][0][None], (128, 2, 128))).astype(f32)
    shared["w_ba"] = inp["w_branch_a"][0]; shared["w_bb"] = inp["w_branch_b"][0]; shared["w_o"] = inp["w_out"][0]
    shared["w_ff1"] = inp["w_ff1"][0]; shared["w_ff2"] = inp["w_ff2"][0]
    per_hf = [host_tables(hf, consts) for hf in range(2)]
    in_maps = []
    for core in range(8):
        b, hf = core // 2, core % 2
        m = dict(shared); m.update(per_hf[hf])
        for nm in ("w_ada", "w_in", "w_ba", "w_bb", "w_o", "w_ff1", "w_ff2", "rowb", "T3"):
            a = shared[nm]
            pad = np.full((1,) + a.shape[1:], core, dtype=a.dtype)
            m[nm] = np.concatenate([a, pad], axis=0)
        m["xp"] = np.ascontiguousarray(np.roll(inp["x"][b], -hf * HALF, axis=0))
        m["ctx"] = np.ascontiguousarray(inp["ctx"][b])
        m["cvec"] = np.ascontiguousarray(np.stack([col(inp["c"][b], 16), col(inp["c_ctx"], 16)], axis=2))
        in_maps.append(m)
    return in_maps


def kernel(**inputs):
    if _NC[0] is None:
        _NC[0] = build_nc()[0]
    nc = _NC[0]
    in_maps = make_in_maps(inputs)
    res = run_bass_kernel_spmd(nc, in_maps, core_ids=list(range(8)))
    out = np.empty((NB, L, D), np.float32)
    for core in range(8):
        b, hf = core // 2, core % 2
        out[b, hf * HALF:(hf + 1) * HALF] = res.results[core]["out"]
    return out
```

```python
import math
from contextlib import ExitStack
import numpy as np
import ml_dtypes
import concourse.bass as bass
import concourse.mybir as mybir
from concourse.bass_utils import run_bass_kernel_spmd

F32 = mybir.dt.float32
BF16 = mybir.dt.bfloat16
I32 = mybir.dt.int32
AF = mybir.ActivationFunctionType
ALU = mybir.AluOpType
AX = mybir.AxisListType

D = 2048; L = 4096; NB = 4; CTX = 256; HALF = 2048
C = 1024; INW = 10240; DFF = 8192
Q_OFF = 3072; K_OFF = 5120; V_OFF = 5632; GA_OFF = 6144; GB_OFF = 8192
NKEY = L + CTX
EPS = 1e-6
STOP_AFTER = 99
DEBUG = False


class Buf:
    def __init__(self, name, dram=False):
        self.name = name; self.w = None; self.r = {}; self.dram = dram


class Trk:
    def __init__(self, nc, es):
        self.nc = nc; self.es = es
        self.eng = {'pe': nc.tensor, 'act': nc.scalar, 'dve': nc.vector, 'pool': nc.gpsimd, 'sp': nc.sync}
        self.sem = {}; self.cnt = {}
        for e in ('pe', 'act', 'dve', 'pool'):
            self.sem[e] = es.enter_context(nc.semaphore("s_" + e)); self.cnt[e] = 0
        self.waited = {}
        self.dsem = {}; self.dcnt = {}
        self.free_dsems = []
        self.all_dma_toks = {}

    def wait(self, e, tok):
        if tok is None:
            return
        key, val = tok
        if key == e and e == 'pe':
            return
        if self.waited.get((e, key), 0) >= val:
            return
        self.waited[(e, key)] = val
        sem = self.sem[key] if key in self.sem else self.dsem[key]
        self.eng[e].wait_ge(sem, val)

    def _pre(self, e, reads, writes):
        for b in reads:
            self.wait(e, b.w)
        for b in writes:
            self.wait(e, b.w)
            for kk_, vv_ in list(b.r.items()):
                self.wait(e, (kk_, vv_))

    def _post(self, tok, reads, writes):
        for b in writes:
            b.w = tok; b.r = {}
        for b in reads:
            if b not in writes:
                b.r[tok[0]] = max(b.r.get(tok[0], 0), tok[1])

    def op(self, e, fn, reads=(), writes=()):
        self._pre(e, reads, writes)
        ins = fn(self.eng[e])
        self.cnt[e] += 1
        ins.then_inc(self.sem[e], 1)
        tok = (e, self.cnt[e])
        self._post(tok, reads, writes)
        return tok

    def dma(self, q, out, in_, reads=(), writes=(), key=None):
        self._pre(q, reads, writes)
        if key is None:
            srcs = [b for b in reads if not b.dram]
            key = ("st_" + srcs[0].name) if (writes[0].dram and srcs) else writes[0].name
        if key not in self.dsem:
            self.dsem[key] = self.es.enter_context(self.nc.semaphore("d_" + key)); self.dcnt[key] = 0
        self.dcnt[key] += 16
        self.eng[q].dma_start(out=out, in_=in_).then_inc(self.dsem[key], 16)
        tok = (key, self.dcnt[key])
        self.all_dma_toks[key] = tok
        self._post(tok, reads, writes)
        return tok

    def alias(self, to_list, from_list):
        for t in to_list:
            for f in from_list:
                if f.w is not None:
                    t.r[f.w[0]] = max(t.r.get(f.w[0], 0), f.w[1])
                for kk_, vv_ in f.r.items():
                    t.r[kk_] = max(t.r.get(kk_, 0), vv_)

    def barrier(self):
        toks = [(e, self.cnt[e]) for e in ('pe', 'act', 'dve', 'pool') if self.cnt[e] > 0]
        toks += list(self.all_dma_toks.values())
        for e in ('pe', 'act', 'dve', 'pool', 'sp'):
            for t in toks:
                if t[0] == e:
                    continue
                self.wait(e, t)


def build_nc():
    nc = bass.Bass("TRN2", target_bir_lowering=False)

    def din(name, shape, dt=F32):
        return nc.dram_tensor(name, list(shape), dt, kind="ExternalInput").ap()

    def dscr(name, shape, dt=BF16):
        return nc.dram_tensor(name, list(shape), dt, kind="Internal").ap()

    xp = din("xp", [L, D]); ctx = din("ctx", [CTX, D]); cvec = din("cvec", [128, 16, 2])
    w_ada = din("w_ada", [D + 1, 6 * D])[0:D, :]; bada = din("bada", [128, 96]); rowb = din("rowb", [129, 4, D])[0:128]
    gpre = din("gpre", [128, 2, 16]); w_in = din("w_in", [D + 1, INW])[0:D, :]
    convw = din("convw", [128, 24, 3]); convb = din("convb", [128, 24])
    embT = din("embT", [17, 2 * L]); fw1 = din("fw1", [17, 64]); fvec = din("fvec", [128, 4])
    fw2 = din("fw2", [64, 64]); fw3d = din("fw3d", [64, 128]); fw4 = din("fw4", [128, C])
    negt = din("negt", [128, 64]); delta = din("delta", [128, C]); fbias = din("fbias", [128, 8])
    qkg = din("qkg", [128, 2, 128]); ropec = din("ropec", [128, 32, 128], BF16); ropes = din("ropes", [128, 32, 128], BF16)
    F1d = din("F1d", [64, 130], BF16); F1k = din("F1k", [128, 130], BF16)
    T3 = din("T3", [129, 65, 4, 64], BF16)[0:128]; E64 = din("E64", [128, 128], BF16); Rt = din("Rt", [65, 64, 2, 32], BF16)
    ident_d = din("ident", [128, 128], BF16); halo = din("halo", [128, 2])
    w_ba = din("w_ba", [C + 1, D])[0:C, :]; w_bb = din("w_bb", [D + 1, D])[0:D, :]; w_o = din("w_o", [D + 1, D])[0:D, :]
    w_ff1 = din("w_ff1", [D + 1, DFF])[0:D, :]; w_ff2 = din("w_ff2", [DFF + 1, D])[0:DFF, :]
    out = nc.dram_tensor("out", [HALF, D], F32, kind="ExternalOutput").ap()

    uT = dscr("uT", [3 * C, L])
    gT = dscr("gT", [2 * D, HALF])
    qT = dscr("qT", [16, 128, HALF])
    kT = dscr("kT", [4, 128, NKEY])
    vtok = dscr("vtok", [NKEY, 512])
    yaT = dscr("yaT", [C, HALF])
    ybT = dscr("ybT", [D, HALF])
    x1s = dscr("x1s", [HALF, D], F32)
    wf1b = dscr("wf1b", [D, DFF]); wf2b = dscr("wf2b", [DFF, D])
    winb = dscr("winb", [D, INW]); wbab = dscr("wbab", [C, D]); wbbb = dscr("wbbb", [D, D]); wob = dscr("wob", [D, D])
    dbg = {}
    if DEBUG:
        for nm, shp, dt in (("d_mod", [128, 96, 2], F32), ("d_hid", [128, 2 * L], BF16), ("d_uT", [3 * C, L], BF16),
                            ("d_qT", [16, 128, HALF], BF16), ("d_kT", [4, 128, NKEY], BF16), ("d_v", [NKEY, 512], BF16),
                            ("d_gT", [2 * D, HALF], BF16), ("d_ya", [C, HALF], BF16), ("d_yb", [D, HALF], BF16),
                            ("d_x1", [HALF, D], F32)):
            dbg[nm] = nc.dram_tensor(nm, shp, dt, kind="ExternalOutput").ap()

    with ExitStack() as es:
        T = Trk(nc, es)
        sb = lambda name, shape, dt=F32, st=es: st.enter_context(nc.sbuf_tensor("sb_" + name, list(shape), dt))
        pf = [es.enter_context(nc.psum_tensor("pf%d" % i, [128, 512], F32)) for i in range(5)]
        pb = [es.enter_context(nc.psum_tensor("pb%d" % i, [128, 1024], BF16)) for i in range(3)]
        pfB = [Buf("pf%d" % i) for i in range(5)]; pbB = [Buf("pb%d" % i) for i in range(3)]
        pfi = [0]; pbi = [0]

        def npf():
            i = pfi[0] % 5; pfi[0] += 1; return pf[i], pfB[i]

        def npb():
            i = pbi[0] % 3; pbi[0] += 1; return pb[i], pbB[i]

        ident = sb("ident", [128, 128], BF16); identB = Buf("ident")
        T.dma('sp', ident[:], ident_d, writes=[identB])
        ones_f = sb("ones_f", [128, 128]); onesB = Buf("ones_f")
        T.op('pool', lambda e: e.memset(ones_f[:], 1.0), writes=[onesB])
        ones_b = sb("ones_b", [128, 128], BF16); onesbB = Buf("ones_b")
        T.op('pool', lambda e: e.memset(ones_b[:], 1.0), writes=[onesbB])
        modv = sb("modv", [128, 6, 16]); modB = Buf("modv")
        Grow = sb("Grow", [128, 2, D]); GrowB = Buf("Grow")
        epsT = sb("epsT", [128, 1]); epsB = Buf("epsT")
        T.op('pool', lambda e: e.memset(epsT[:], EPS), writes=[epsB])

        sH = ExitStack()
        hid = sb("hid", [128, 2 * L], BF16, sH); hidB = Buf("hid")
        wf1B = Buf("wf1b", dram=True); wf2B = Buf("wf2b", dram=True)
        s2b = ExitStack()
        if True:
            cb = [sb("cb%d" % i, [128, 8192], BF16, s2b) for i in range(2)]; cbB = [Buf("cb%d" % i) for i in range(2)]
            it = 0
            winB = Buf("winb", dram=True); wmB = Buf("wmerge", dram=True)
            for k in range(16):
                for hcol in range(2):
                    i = it % 2; it += 1
                    T.dma('pool', cb[i][:, 0:5120], w_in[k * 128:(k + 1) * 128, hcol * 5120:(hcol + 1) * 5120], writes=[cbB[i]])
                    T.dma('sp', winb[k * 128:(k + 1) * 128, hcol * 5120:(hcol + 1) * 5120], cb[i][:, 0:5120], reads=[cbB[i]], writes=[winB])
            for src_, dst_, nk in ((w_ba, wbab, 2), (w_bb, wbbb, 4), (w_o, wob, 4)):
                for k in range(nk):
                    i = it % 2; it += 1
                    T.dma('pool', cb[i][:].rearrange("p (a c) -> p a c", a=4),
                          src_[k * 512:(k + 1) * 512, :].rearrange("(a p) c -> p a c", p=128), writes=[cbB[i]])
                    T.dma('sp', dst_[k * 512:(k + 1) * 512, :].rearrange("(a p) c -> p a c", p=128),
                          cb[i][:].rearrange("p (a c) -> p a c", a=4), reads=[cbB[i]], writes=[wmB])
        with ExitStack() as s1:
            s1.enter_context(nc.named_scope("s1"))
            cv = sb("cv", [128, 16, 2], F32, s1); cvB = Buf("cv")
            T.dma('act', cv[:], cvec, writes=[cvB])
            sl = sb("sl", [128, 16, 2], F32, s1); slB = Buf("sl")
            T.op('act', lambda e: e.activation(out=sl[:], in_=cv[:], func=AF.Silu), reads=[cvB], writes=[slB])
            bad = sb("bad", [128, 96], F32, s1); badB = Buf("bad")
            T.dma('act', bad[:], bada, writes=[badB])
            gp = sb("gp", [128, 2, 16], F32, s1); gpB = Buf("gp")
            T.dma('act', gp[:], gpre, writes=[gpB])
            rb = sb("rb", [128, 4, D], F32, s1); rbB = Buf("rb")
            T.dma('act', rb[:], rowb, writes=[rbB])
            modT = sb("modT", [128, 96, 2], F32, s1); modTB = Buf("modT")
            identf = sb("identf", [128, 128], F32, s1); identfB = Buf("identf")
            T.op('dve', lambda e: e.tensor_copy(out=identf[:], in_=ident[:]), reads=[identB], writes=[identfB])
            wcb = [sb("wcb%d" % i, [128, 16, 256], F32, s1) for i in range(2)]; wcbB = [Buf("wcb%d" % i) for i in range(2)]
            mrow = sb("mrow", [2, 6 * D], F32, s1); mrowB = Buf("mrow")
            for cb_ in range(48):
                w_, wB_ = wcb[cb_ % 2], wcbB[cb_ % 2]
                T.dma('act', w_[:], w_ada[:, cb_ * 256:(cb_ + 1) * 256].rearrange("(k p) c -> p k c", p=128), writes=[wB_])
                ps, psB = npf()

                def f(e, ps=ps, w_=w_):
                    for k in range(16):
                        ins = e.matmul(ps[0:2, 0:256], lhsT=sl[:, k, :], rhs=w_[:, k, :], start=(k == 0), stop=(k == 15))
                    return ins
                T.op('pe', f, reads=[wB_, slB], writes=[psB])
                T.op('dve', lambda e, ps=ps, cb_=cb_: e.tensor_copy(out=mrow[:, cb_ * 256:(cb_ + 1) * 256], in_=ps[0:2, 0:256]),
                     reads=[psB], writes=[mrowB])
            ps, psB = npf()

            def ft(e, ps=ps):
                for j in range(96):
                    ins = e.transpose(ps[:, 2 * j:2 * j + 2], mrow[0:2, j * 128:(j + 1) * 128], identf[0:2, 0:2])
                return ins
            T.op('pe', ft, reads=[mrowB, identfB], writes=[psB])
            T.op('dve', lambda e, ps=ps: e.tensor_tensor(
                out=modT[:], in0=ps[:, 0:192].rearrange("p (j t) -> p j t", t=2),
                in1=bad[:].unsqueeze(2).to_broadcast([128, 96, 2]), op=ALU.add), reads=[psB, badB], writes=[modTB])
            for hfc in range(2):
                base = 2 * D if hfc == 0 else 5 * D
                for bl in range(4):
                    pr, prB = npf()
                    T.op('pe', lambda e, pr=pr, base=base, bl=bl: e.matmul(
                        pr[:], lhsT=ones_f[0:1, :], rhs=mrow[0:1, base + bl * 512:base + (bl + 1) * 512], start=True, stop=True),
                        reads=[mrowB, onesB], writes=[prB])
                    T.op('dve', lambda e, pr=pr, bl=bl, hfc=hfc: e.tensor_tensor(
                        out=Grow[:, hfc, bl * 512:(bl + 1) * 512], in0=pr[:], in1=rb[:, hfc, bl * 512:(bl + 1) * 512],
                        op=ALU.add), reads=[prB, rbB], writes=[GrowB])
                T.op('dve', lambda e, hfc=hfc: e.tensor_tensor(out=Grow[:, hfc, :], in0=Grow[:, hfc, :],
                                                              in1=rb[:, 2 + hfc, :], op=ALU.mult),
                     reads=[rbB], writes=[GrowB])
            T.op('dve', lambda e: e.scalar_tensor_tensor(out=modv[:, 0, :], in0=modT[:, 16:32, 0], scalar=1.0,
                                                         in1=gp[:, 0, :], op0=ALU.add, op1=ALU.mult),
                 reads=[modTB, gpB], writes=[modB])
            T.op('dve', lambda e: e.tensor_copy(out=modv[:, 1, :], in_=modT[:, 0:16, 0]), reads=[modTB], writes=[modB])
            T.op('dve', lambda e: e.scalar_tensor_tensor(out=modv[:, 2, :], in0=modT[:, 16:32, 1], scalar=1.0,
                                                         in1=gp[:, 0, :], op0=ALU.add, op1=ALU.mult),
                 reads=[modTB, gpB], writes=[modB])
            T.op('dve', lambda e: e.tensor_copy(out=modv[:, 3, :], in_=modT[:, 0:16, 1]), reads=[modTB], writes=[modB])
            T.op('dve', lambda e: e.scalar_tensor_tensor(out=modv[:, 4, :], in0=modT[:, 64:80, 0], scalar=1.0,
                                                         in1=gp[:, 1, :], op0=ALU.add, op1=ALU.mult),
                 reads=[modTB, gpB], writes=[modB])
            T.op('dve', lambda e: e.tensor_copy(out=modv[:, 5, :], in_=modT[:, 48:64, 0]), reads=[modTB], writes=[modB])
            if DEBUG:
                T.dma('act', dbg["d_mod"], modT[:], reads=[modTB], writes=[Buf("dbg_mod", dram=True)])
            T.barrier()
        if STOP_AFTER <= 1:
            sH.close()
            return nc, T

        with ExitStack() as s2:
            s2.enter_context(nc.named_scope("s2"))
            emb = sb("emb", [17, 2 * L], F32, s2); embB = Buf("emb")
            T.dma('act', emb[:], embT, writes=[embB])
            w1 = sb("w1", [17, 64], F32, s2); w2 = sb("w2", [64, 64], F32, s2); w3 = sb("w3", [64, 128], F32, s2)
            fv = sb("fv", [128, 4], F32, s2); fwB = Buf("fw")
            T.dma('act', w1[:], fw1, writes=[fwB]); T.dma('act', w2[:], fw2, writes=[fwB])
            T.dma('act', w3[:], fw3d, writes=[fwB]); T.dma('act', fv[:], fvec, writes=[fwB])
            fsc = sb("fsc", [128, 4], F32, s2); fscB = Buf("fsc")
            T.op('dve', lambda e: e.tensor_scalar(out=fsc[:, 3:4], in0=fv[:, 3:4], scalar1=1.0 / (2 * math.pi), scalar2=None,
                                                  op0=ALU.mult), reads=[fwB], writes=[fscB])
            T.op('dve', lambda e: e.tensor_scalar(out=fsc[:, 0:3], in0=fv[:, 0:3], scalar1=fsc[:, 3:4], scalar2=None,
                                                  op0=ALU.mult), reads=[fwB, fscB], writes=[fscB])
            h1 = sb("h1", [64, 2 * L], F32, s2); h1B = Buf("h1")
            h2 = sb("h2", [64, 2 * L], F32, s2); h2B = Buf("h2")
            tv = [sb("tv%d" % i, [128, 512], F32, s2) for i in range(2)]; tvB = [Buf("tv%d" % i) for i in range(2)]
            ti = [sb("ti%d" % i, [128, 512], I32, s2) for i in range(2)]; tiB = [Buf("ti%d" % i) for i in range(2)]
            tf = [sb("tf%d" % i, [128, 512], F32, s2) for i in range(2)]; tfB = [Buf("tf%d" % i) for i in range(2)]
            it = 0
            for layer in range(3):
                src, srcB, wgt, kk, mm = ((emb, embB, w1, 17, 64), (h1, h1B, w2, 64, 64), (h2, h2B, w3, 64, 128))[layer]
                for ch in range(16):
                    ps, psB = npf()
                    T.op('pe', lambda e, ps=ps, src=src, wgt=wgt, kk=kk, mm=mm, ch=ch: e.matmul(
                        ps[0:mm, :], lhsT=wgt[0:kk, 0:mm], rhs=src[0:kk, ch * 512:(ch + 1) * 512], start=True, stop=True),
                        reads=[srcB, fwB], writes=[psB])
                    i = it % 2; it += 1
                    T.op('dve', lambda e, ps=ps, i=i, mm=mm, layer=layer: e.tensor_scalar(
                        out=tv[i][0:mm, :], in0=ps[0:mm, :], scalar1=fsc[0:mm, 3:4], scalar2=fsc[0:mm, layer:layer + 1],
                        op0=ALU.mult, op1=ALU.add), reads=[psB, fscB], writes=[tvB[i]])
                    T.op('dve', lambda e, i=i, mm=mm: e.tensor_copy(out=ti[i][0:mm, :], in_=tv[i][0:mm, :]),
                         reads=[tvB[i]], writes=[tiB[i]])
                    T.op('dve', lambda e, i=i, mm=mm: e.tensor_copy(out=tf[i][0:mm, :], in_=ti[i][0:mm, :]),
                         reads=[tiB[i]], writes=[tfB[i]])
                    T.op('dve', lambda e, i=i, mm=mm: e.tensor_tensor(out=tv[i][0:mm, :], in0=tv[i][0:mm, :],
                                                                       in1=tf[i][0:mm, :], op=ALU.subtract),
                         reads=[tfB[i]], writes=[tvB[i]])
                    dst, dstB = ((h1, h1B), (h2, h2B), (hid, hidB))[layer]
                    T.op('act', lambda e, i=i, mm=mm, dst=dst, ch=ch: e.activation(
                        out=dst[0:mm, ch * 512:(ch + 1) * 512], in_=tv[i][0:mm, :], func=AF.Sin, scale=2 * math.pi),
                        reads=[tvB[i]], writes=[dstB])
            T.op('pool', lambda e: e.memset(hid[0:64, L:2 * L], 0.0), writes=[hidB])
            T.op('pool', lambda e: e.memset(hid[64:128, 0:L + 1], 0.0), writes=[hidB])
            if DEBUG:
                T.dma('act', dbg["d_hid"], hid[:], reads=[hidB], writes=[Buf("dbg_hid", dram=True)])
            T.barrier()
        s2b.close()
        if STOP_AFTER <= 2:
            sH.close()
            return nc, T

        uTB = Buf("uT", dram=True); gTB = Buf("gT", dram=True); qTB = Buf("qT", dram=True); kTB = Buf("kT", dram=True); vB = Buf("vtok", dram=True)
        with ExitStack() as s3:
            s3.enter_context(nc.named_scope("s3"))
            rc = sb("rc", [128, 32, 128], BF16, s3); rs = sb("rs", [128, 32, 128], BF16, s3); ropeB = Buf("rope")
            T.dma('sp', rc[:], ropec, writes=[ropeB]); T.dma('sp', rs[:], ropes, writes=[ropeB])
            qg = sb("qg", [128, 2, 128], F32, s3); qgB = Buf("qg")
            T.dma('sp', qg[:], qkg, writes=[qgB])
            hT = sb("hT", [128, 16, 1024], BF16, s3); hTB = Buf("hT")
            xt = [sb("xt%d" % i, [128, D], F32, s3) for i in range(2)]; xtB = [Buf("xt%d" % i) for i in range(2)]
            xn = [sb("xn%d" % i, [128, D], BF16, s3) for i in range(2)]; xnB = [Buf("xn%d" % i) for i in range(2)]
            junk = sb("junk", [128, D], BF16, s3); junkB = Buf("junk")
            st1 = [sb("st1_%d" % i, [128, 4], F32, s3) for i in range(2)]; st1B = [Buf("st1_%d" % i) for i in range(2)]
            wt = [sb("wt%d" % i, [128, 16, 512], BF16, s3) for i in range(3)]; wtB = [Buf("wt%d" % i) for i in range(3)]
            cst = [sb("cst%d" % i, [128, 1024], BF16, s3) for i in range(2)]; cstB = [Buf("cst%d" % i) for i in range(2)]
            qs = [sb("qs%d" % i, [128, 512], F32, s3) for i in range(4)]; qsB = [Buf("qs%d" % i) for i in range(4)]
            qsq = sb("qsq", [128, 512], F32, s3); qsqB = Buf("qsq")
            qn = [sb("qn%d" % i, [128, 512], F32, s3) for i in range(4)]; qnB = [Buf("qn%d" % i) for i in range(4)]
            qr = [sb("qr%d" % i, [128, 512], F32, s3) for i in range(4)]; qrB = [Buf("qr%d" % i) for i in range(4)]
            qb = [sb("qb%d" % i, [128, 512], BF16, s3) for i in range(4)]; qbB = [Buf("qb%d" % i) for i in range(4)]
            stq = [sb("stq%d" % i, [128, 4], F32, s3) for i in range(4)]; stqB = [Buf("stq%d" % i) for i in range(4)]
            qtt = [sb("qtt%d" % i, [128, 4, 128], BF16, s3) for i in range(4)]; qttB = [Buf("qtt%d" % i) for i in range(4)]
            ctr = {'x': 0, 'wk': 0, 'wt': 0, 'cst': 0, 'q': 0}

            def build_hT(src_ap, ntiles, gi, si):
                for tl in range(ntiles):
                    i = ctr['x'] % 2; ctr['x'] += 1
                    T.dma('sp', xt[i][:], src_ap[tl * 128:(tl + 1) * 128, :], writes=[xtB[i]])
                    T.op('act', lambda e, i=i: e.activation(out=junk[:], in_=xt[i][:], func=AF.Square,
                                                            accum_out=st1[i][:, 0:1]),
                         reads=[xtB[i]], writes=[junkB, st1B[i]])
                    T.op('act', lambda e, i=i: e.activation(out=st1[i][:, 1:2], in_=st1[i][:, 0:1], func=AF.Sqrt,
                                                            bias=epsT[:, 0:1], scale=1.0 / D),
                         reads=[epsB], writes=[st1B[i]])
                    T.op('dve', lambda e, i=i: e.reciprocal(out=st1[i][:, 2:3], in_=st1[i][:, 1:2]), writes=[st1B[i]])
                    T.op('dve', lambda e, i=i: e.tensor_scalar(out=xn[i][:], in0=xt[i][:], scalar1=st1[i][:, 2:3],
                                                              scalar2=None, op0=ALU.mult),
                         reads=[xtB[i], st1B[i]], writes=[xnB[i]])
                    for hb in range(2):
                        p_, pB_ = npb()

                        def f(e, p_=p_, i=i, hb=hb):
                            for j in range(8):
                                k = hb * 8 + j
                                ins = e.transpose(p_[:, j * 128:(j + 1) * 128], xn[i][:, k * 128:(k + 1) * 128], ident[:])
                            return ins
                        T.op('pe', f, reads=[xnB[i], identB], writes=[pB_])
                        for j in range(8):
                            k = hb * 8 + j
                            T.op('act', lambda e, p_=p_, j=j, k=k, tl=tl: e.activation(
                                out=hT[:, k, tl * 128:(tl + 1) * 128], in_=p_[:, j * 128:(j + 1) * 128], func=AF.Identity,
                                bias=modv[:, si, k:k + 1], scale=modv[:, gi, k:k + 1]),
                                reads=[pB_, modB], writes=[hTB])

            pre = {}

            def preload(col):
                i = ctr['wt'] % 3; ctr['wt'] += 1
                T.dma('sp', wt[i][:], winb[:, col:col + 512].rearrange("(k p) c -> p k c", p=128), reads=[winB], writes=[wtB[i]])
                pre.setdefault(col, []).append(i)

            def getw(col):
                if not pre.get(col):
                    preload(col)
                return pre[col].pop(0)

            def chan_major4(col, ntok, dst_fn, dstB, sig=False):
                i = getw(col)
                for q in range(4):
                    ci = ctr['cst'] % 2; ctr['cst'] += 1
                    for sbk in range(ntok // 512):
                        ps, psB = npf()

                        def f(e, ps=ps, i=i, sbk=sbk, q=q):
                            for k in range(16):
                                ins = e.matmul(ps[:], lhsT=wt[i][:, k, q * 128:(q + 1) * 128], rhs=hT[:, k, sbk * 512:(sbk + 1) * 512],
                                               start=(k == 0), stop=(k == 15))
                            return ins
                        T.op('pe', f, reads=[wtB[i], hTB], writes=[psB])
                        T.op('act', lambda e, ps=ps, ci=ci, sbk=sbk: e.activation(
                            out=cst[ci][:, sbk * 512:(sbk + 1) * 512], in_=ps[:], func=(AF.Sigmoid if sig else AF.Copy)),
                            reads=[psB], writes=[cstB[ci]])
                    T.dma('sp', dst_fn(q), cst[ci][:, 0:ntok], reads=[cstB[ci]], writes=[dstB])

            def tok_major(col, ntiles, kind, tok0, gidx=0, rope=True):
                i = getw(col)
                for tl in range(ntiles):
                    ps, psB = npf()

                    def f(e, ps=ps, i=i, tl=tl):
                        for k in range(16):
                            ins = e.matmul(ps[:], lhsT=hT[:, k, tl * 128:(tl + 1) * 128], rhs=wt[i][:, k, :],
                                           start=(k == 0), stop=(k == 15))
                        return ins
                    T.op('pe', f, reads=[wtB[i], hTB], writes=[psB])
                    j = ctr['q'] % 4; ctr['q'] += 1
                    t0 = tok0 + tl * 128
                    if kind == 'v':
                        T.op('act', lambda e, ps=ps, j=j: e.activation(out=qb[j][:], in_=ps[:], func=AF.Copy),
                             reads=[psB], writes=[qbB[j]])
                        T.dma('sp', vtok[t0:t0 + 128, :], qb[j][:], reads=[qbB[j]], writes=[vB])
                        continue
                    T.op('act', lambda e, ps=ps, j=j: e.activation(out=qs[j][:], in_=ps[:], func=AF.Copy),
                         reads=[psB], writes=[qsB[j]])
                    T.op('dve', lambda e, j=j: e.tensor_tensor(out=qsq[:], in0=qs[j][:], in1=qs[j][:], op=ALU.mult),
                         reads=[qsB[j]], writes=[qsqB])
                    sj = stq[j]
                    T.op('dve', lambda e, sj=sj: e.tensor_reduce(out=sj[:, 0:4], in_=qsq[:].rearrange("p (h d) -> p h d", d=128),
                                                                 axis=AX.X, op=ALU.add), reads=[qsqB], writes=[stqB[j]])
                    T.op('act', lambda e, sj=sj: e.activation(out=sj[:, 0:4], in_=sj[:, 0:4], func=AF.Sqrt,
                                                              bias=epsT[:, 0:1], scale=1.0 / 128),
                         reads=[epsB], writes=[stqB[j]])
                    T.op('dve', lambda e, sj=sj: e.reciprocal(out=sj[:, 0:4], in_=sj[:, 0:4]), writes=[stqB[j]])
                    gsel = 0 if kind == 'q' else 1
                    for h in range(4):
                        T.op('dve', lambda e, j=j, h=h, sj=sj: e.scalar_tensor_tensor(
                            out=qn[j][:, h * 128:(h + 1) * 128], in0=qs[j][:, h * 128:(h + 1) * 128], scalar=sj[:, h:h + 1],
                            in1=qg[:, gsel, :], op0=ALU.mult, op1=ALU.mult),
                            reads=[qsB[j], stqB[j], qgB], writes=[qnB[j]])
                    if rope:
                        ti_ = t0 // 128
                        qn4 = qn[j][:].rearrange("p (h d) -> p h d", d=128)
                        qr4 = qr[j][:].rearrange("p (h d) -> p h d", d=128)
                        cb_ = rc[:, ti_, :].unsqueeze(1).to_broadcast([128, 4, 128])
                        qn5 = qn[j][:].rearrange("p (h a t d) -> p h a t d", a=2, t=2, d=32)
                        qr5 = qr[j][:].rearrange("p (h a t d) -> p h a t d", a=2, t=2, d=32)
                        rs5 = rs[:, ti_, :].rearrange("p (a t d) -> p a t d", a=2, t=2)
                        for h in range(4):
                            for tt in range(2):
                                T.op('pool', lambda e, h=h, tt=tt, qn5=qn5, qr5=qr5, rs5=rs5: e.tensor_tensor(
                                    out=qr5[:, h, :, tt, :], in0=qn5[:, h, :, 1 - tt, :], in1=rs5[:, :, tt, :], op=ALU.mult),
                                    reads=[qnB[j], ropeB], writes=[qrB[j]])
                        T.op('dve', lambda e, qn4=qn4, cb_=cb_: e.tensor_tensor(out=qn4, in0=qn4, in1=cb_, op=ALU.mult),
                             reads=[ropeB], writes=[qnB[j]])
                        T.op('dve', lambda e, j=j: e.tensor_tensor(out=qb[j][:], in0=qn[j][:], in1=qr[j][:], op=ALU.add),
                             reads=[qnB[j], qrB[j]], writes=[qbB[j]])
                    else:
                        T.op('dve', lambda e, j=j: e.tensor_copy(out=qb[j][:], in_=qn[j][:]), reads=[qnB[j]], writes=[qbB[j]])
                    p_, pB_ = npb()

                    def f2(e, p_=p_, j=j):
                        for h in range(4):
                            ins = e.transpose(p_[:, h * 128:(h + 1) * 128], qb[j][:, h * 128:(h + 1) * 128], ident[:])
                        return ins
                    T.op('pe', f2, reads=[qbB[j], identB], writes=[pB_])
                    T.op('act', lambda e, p_=p_, j=j: e.activation(out=qtt[j][:], in_=p_[:, 0:512].rearrange(
                        "p (h t) -> p h t", t=128), func=AF.Copy), reads=[pB_], writes=[qttB[j]])
                    if kind == 'q':
                        T.dma('sp', qT[gidx * 4:(gidx + 1) * 4, :, t0:t0 + 128].rearrange("h d t -> d h t"), qtt[j][:],
                              reads=[qttB[j]], writes=[qTB])
                    else:
                        T.dma('sp', kT[:, :, t0:t0 + 128].rearrange("h d t -> d h t"), qtt[j][:],
                              reads=[qttB[j]], writes=[kTB])

            calls = []
            for blk in range(4):
                own = blk < 2
                calls.append(('h', None, lambda blk=blk: build_hT(xp[blk * 1024:(blk + 1) * 1024, :], 8, 0, 1)))
                for c4 in range(6):
                    calls.append(('w', c4 * 512, lambda c4=c4, blk=blk: chan_major4(
                        c4 * 512, 1024, lambda q: uT[c4 * 512 + q * 128:c4 * 512 + (q + 1) * 128, blk * 1024:(blk + 1) * 1024], uTB)))
                if own:
                    for c4 in range(8):
                        calls.append(('w', GA_OFF + c4 * 512, lambda c4=c4, blk=blk: chan_major4(
                            GA_OFF + c4 * 512, 1024, lambda q: gT[c4 * 512 + q * 128:c4 * 512 + (q + 1) * 128,
                                                                  blk * 1024:(blk + 1) * 1024], gTB, sig=True)))
                    for g in range(4):
                        calls.append(('w', Q_OFF + g * 512, lambda g=g, blk=blk: tok_major(Q_OFF + g * 512, 8, 'q', blk * 1024, gidx=g)))
                calls.append(('w', K_OFF, lambda blk=blk: tok_major(K_OFF, 8, 'k', blk * 1024)))
                calls.append(('w', V_OFF, lambda blk=blk: tok_major(V_OFF, 8, 'v', blk * 1024)))
            calls.append(('h', None, lambda: build_hT(ctx, 2, 2, 3)))
            calls.append(('w', K_OFF, lambda: tok_major(K_OFF, 2, 'k', L, rope=False)))
            calls.append(('w', V_OFF, lambda: tok_major(V_OFF, 2, 'v', L)))
            wcols = [c_[1] for c_ in calls if c_[0] == 'w']
            wi_ = 0
            preload(wcols[0])
            for kind_, col_, fn_ in calls:
                if kind_ == 'w':
                    wi_ += 1
                    if wi_ < len(wcols):
                        preload(wcols[wi_])
                fn_()
            if DEBUG:
                T.barrier()
                for nm, src, sB in (("d_uT", uT, uTB), ("d_qT", qT, qTB), ("d_kT", kT, kTB), ("d_v", vtok, vB), ("d_gT", gT, gTB)):
                    T.dma('sp', dbg[nm], src, reads=[sB], writes=[Buf("dbg_" + nm)])
            T.barrier()
        if STOP_AFTER <= 3:
            sH.close()
            return nc, T

        yaB = Buf("yaT", dram=True)
        with ExitStack() as s4:
            s4.enter_context(nc.named_scope("s4"))
            f1d = sb("f1d", [64, 130], BF16, s4); f1k = sb("f1k", [128, 130], BF16, s4)
            t3 = sb("t3", [128, 65, 4, 64], BF16, s4); e64 = sb("e64", [128, 128], BF16, s4)
            rt = sb("rt", [65, 64, 2, 32], BF16, s4); tabB = Buf("fft_tabs")
            for dst_, src_ in ((f1d, F1d), (f1k, F1k), (t3, T3), (e64, E64), (rt, Rt)):
                T.dma('sp', dst_[:], src_, writes=[tabB])
            ngt = sb("ngt", [128, 64], F32, s4); cw = sb("cw", [128, 24, 3], F32, s4); cbv = sb("cbv", [128, 24], F32, s4)
            fbs = sb("fbs", [128, 8], F32, s4); hal = sb("hal", [128, 2], F32, s4); smB = Buf("h_small")
            for dst_, src_ in ((ngt, negt), (cw, convw), (cbv, convb), (fbs, fbias), (hal, halo)):
                T.dma('sp', dst_[:], src_, writes=[smB])
            w4g = sb("w4g", [128, 128], BF16, s4); w4B = Buf("w4g")
            dlt = sb("dlt", [128, 128], F32, s4); dltB = Buf("dlt")
            dec = [sb("dec%d" % i, [128, 4, 128], F32, s4) for i in range(2)]; decB = [Buf("dec%d" % i) for i in range(2)]
            ksum = sb("ksum", [128, 128], F32, s4); ksumB = Buf("ksum")
            rnorm = sb("rnorm", [128, 128], F32, s4); rnB = Buf("rnorm")
            kfz = sb("kfz", [128, 128 * 64], BF16, s4); kfzB = Buf("kfz")
            kf = kfz[:].rearrange("p (c s) -> p c s", s=64)
            zF = kfz[0:64, :].rearrange("p (c s) -> p c s", s=64)
            ysb = kfz[0:32, :].rearrange("p (s c) -> p s c", c=128)
            act_ = sb("act_", [128, 2 * 65 * 128], BF16, s4); AB = Buf("A_CT"); CTB = AB
            A = act_[:, 0:65 * 128].rearrange("p (f c) -> p f c", f=65)
            CT = act_[0:65, 0:2 * 64 * 128].rearrange("p (r s c) -> p r s c", r=2, s=64)
            big3 = sb("big3", [128, 2 * 65 * 128], BF16, s4); KAB = Buf("KA")
            KA = big3[:, 0:8320].rearrange("p (f c) -> p f c", c=128)
            KB_ = big3[:, 8320:16640].rearrange("p (f c) -> p f c", c=128)
            ur = [big3[:, i * 4100:(i + 1) * 4100].rearrange("p (s t) -> p s t", s=2) for i in range(2)]
            urB = [Buf("ur%d" % i) for i in range(2)]
            cv_ = [big3[:, 8200 + i * L:8200 + (i + 1) * L] for i in range(2)]; cvB_ = [Buf("cvh%d" % i) for i in range(2)]
            x0c = sb("x0c", [128, HALF], BF16, s4); x0B = Buf("x0c")
            zbf = sb("zbf", [128, L], BF16, s4); zbB = Buf("zbf")
            tm = [sb("tm%d" % i, [128, 256], F32, s4) for i in range(2)]; tmB = [Buf("tm%d" % i) for i in range(2)]
            Y = sb("Y", [128, 128, 65], BF16, s4); YB = Buf("Y")
            yT = sb("yT", [128, HALF], F32, s4); yTB = Buf("yT")
            yab = sb("yab", [128, HALF], BF16, s4); yabB = Buf("yab")

            def step1(src, kk, f1t):
                for gi_, c0_ in enumerate(range(0, 128, 7)):
                    ncn = min(7, 128 - c0_)
                    ps, psB = npf()

                    def f(e, ps=ps, c0_=c0_, ncn=ncn):
                        for ci in range(ncn):
                            e.matmul(ps[0:64, ci * 65:(ci + 1) * 65], lhsT=src[0:kk, c0_ + ci, :], rhs=f1t[0:kk, 0:65],
                                     start=True, stop=True)
                            ins = e.matmul(ps[64:128, ci * 65:(ci + 1) * 65], lhsT=src[0:kk, c0_ + ci, :], rhs=f1t[0:kk, 65:130],
                                           start=True, stop=True)
                        return ins
                    T.op('pe', f, reads=[kfzB, tabB], writes=[psB])
                    src_ap = ps[:, 0:ncn * 65].rearrange("p (c f) -> p f c", f=65)
                    if gi_ % 2 == 0:
                        T.op('act', lambda e, src_ap=src_ap, c0_=c0_, ncn=ncn: e.activation(
                            out=A[:, :, c0_:c0_ + ncn], in_=src_ap, func=AF.Copy), reads=[psB], writes=[AB])
                    else:
                        T.op('dve', lambda e, src_ap=src_ap, c0_=c0_, ncn=ncn: e.tensor_copy(
                            out=A[:, :, c0_:c0_ + ncn], in_=src_ap), reads=[psB], writes=[AB])

            def step3_data(e, f1, ps):
                e.matmul(ps[:, 0:128], lhsT=t3[:, f1, 0:2, :].rearrange("p a m -> p (a m)"), rhs=A[:, f1, :], start=True, stop=True)
                return e.matmul(ps[:, 128:256], lhsT=t3[:, f1, 1:3, :].rearrange("p a m -> p (a m)"), rhs=A[:, f1, :],
                                start=True, stop=True)

            def step3_filt(e, f1, ps):
                ins = None
                for off, (bt, bb) in ((0, (0, 0)), (128, (3, 1))):
                    e.matmul(ps[0:64, off:off + 128], lhsT=t3[:, f1, bt, :], rhs=A[:, f1, :], start=True, stop=True)
                    ins = e.matmul(ps[64:128, off:off + 128], lhsT=t3[:, f1, bb, :], rhs=A[:, f1, :], start=True, stop=True)
                return ins

            for g in range(8):
                c0 = g * 128
                T.alias(urB + cvB_, [KAB])
                for which in (1, 2, 0):
                    ui = which % 2
                    u_, uB_ = ur[ui], urB[ui]
                    row0 = which * C + c0
                    T.dma('sp', u_[:, :, 1:HALF + 1], uT[row0:row0 + 128, :].rearrange("c (s t) -> c s t", s=2),
                          reads=[uTB], writes=[uB_])
                    for (seg, cell, sseg, scell, fl) in ((0, 0, 1, HALF, 0), (0, HALF + 1, 1, 1, 1), (1, 0, 0, HALF, 1),
                                                         (1, HALF + 1, 0, 1, 0)):
                        T.op('dve', lambda e, u_=u_, seg=seg, cell=cell, sseg=sseg, scell=scell, fl=fl: e.tensor_scalar(
                            out=u_[:, seg, cell:cell + 1], in0=u_[:, sseg, scell:scell + 1], scalar1=hal[:, fl:fl + 1],
                            scalar2=None, op0=ALU.mult), reads=[smB], writes=[uB_])
                    j = which * 8 + g
                    nseg = 1 if which == 0 else 2
                    if which == 0:
                        dstv, dstB_ = x0c, x0B
                    else:
                        dstv, dstB_ = cv_[which - 1], cvB_[which - 1]
                    for seg in range(nseg):
                        o = dstv[:, seg * HALF:(seg + 1) * HALF]
                        eng = 'dve'
                        T.op(eng, lambda e, o=o, u_=u_, seg=seg, j=j: e.tensor_scalar(
                            out=o, in0=u_[:, seg, 1:HALF + 1], scalar1=cw[:, j, 1:2], scalar2=cbv[:, j:j + 1],
                            op0=ALU.mult, op1=ALU.add), reads=[uB_, smB], writes=[dstB_])
                        T.op(eng, lambda e, o=o, u_=u_, seg=seg, j=j: e.scalar_tensor_tensor(
                            out=o, in0=u_[:, seg, 0:HALF], scalar=cw[:, j, 0:1], in1=o, op0=ALU.mult, op1=ALU.add),
                            reads=[uB_, smB], writes=[dstB_])
                        T.op(eng, lambda e, o=o, u_=u_, seg=seg, j=j: e.scalar_tensor_tensor(
                            out=o, in0=u_[:, seg, 2:HALF + 2], scalar=cw[:, j, 2:3], in1=o, op0=ALU.mult, op1=ALU.add),
                            reads=[uB_, smB], writes=[dstB_])
                T.op('dve', lambda e: e.tensor_tensor(out=zbf[:, 0:HALF], in0=cv_[0][:, 0:HALF], in1=cv_[1][:, 0:HALF], op=ALU.mult),
                     reads=[cvB_[0], cvB_[1]], writes=[zbB])
                T.op('pool', lambda e: e.tensor_tensor(out=zbf[:, HALF:L], in0=cv_[0][:, HALF:L], in1=cv_[1][:, HALF:L],
                                                       op=ALU.mult), reads=[cvB_[0], cvB_[1]], writes=[zbB])
                T.alias([KAB], urB + cvB_)
                T.dma('pool', w4g[:], fw4[:, c0:c0 + 128], writes=[w4B])
                T.dma('sp', dlt[:], delta[:, c0:c0 + 128], writes=[dltB])
                hid3 = hid[:].rearrange("p (a b) -> p b a", b=64)
                for q4 in range(16):
                    ps, psB = npf()

                    def f(e, ps=ps, q4=q4):
                        for j in range(4):
                            ins = e.matmul(ps[:, j * 128:(j + 1) * 128], lhsT=hid3[:, q4 * 4 + j, :], rhs=w4g[:],
                                           start=True, stop=True)
                        return ins
                    T.op('pe', f, reads=[hidB, w4B], writes=[psB])
                    di = q4 % 2
                    for j in range(4):
                        s2 = q4 * 4 + j
                        T.op('act', lambda e, di=di, j=j, s2=s2: e.activation(out=dec[di][:, j, :], in_=dlt[:], func=AF.Exp,
                                                                               scale=ngt[:, s2:s2 + 1]),
                             reads=[dltB, smB], writes=[decB[di]])
                    T.op('dve', lambda e, ps=ps, di=di, q4=q4: e.tensor_tensor(
                        out=kf[:, :, q4 * 4:(q4 + 1) * 4].rearrange("p c s -> p s c"),
                        in0=ps[:].rearrange("p (s c) -> p s c", c=128), in1=dec[di][:], op=ALU.mult),
                        reads=[psB, decB[di]], writes=[kfzB])
                T.op('dve', lambda e: e.tensor_reduce(out=ksum[:], in_=kf, axis=AX.X, op=ALU.add, apply_absolute_value=True),
                     reads=[kfzB], writes=[ksumB])
                ps, psB = npf()
                T.op('pe', lambda e, ps=ps: e.matmul(ps[:, 0:128], lhsT=ones_f[:], rhs=ksum[:], start=True, stop=True),
                     reads=[ksumB, onesB], writes=[psB])
                T.op('dve', lambda e, ps=ps: e.reciprocal(out=rnorm[:], in_=ps[:, 0:128]), reads=[psB], writes=[rnB])
                step1(kf, 128, f1k)
                for f1 in range(65):
                    ps, psB = npf()

                    def f(e, ps=ps, f1=f1):
                        return step3_filt(e, f1, ps)
                    T.op('pe', f, reads=[AB, tabB], writes=[psB])
                    T.op('dve', lambda e, ps=ps, f1=f1: e.tensor_tensor(out=KA[:, f1, :], in0=ps[:, 0:128], in1=rnorm[:],
                                                                        op=ALU.mult), reads=[psB, rnB], writes=[KAB])
                    T.op('dve', lambda e, ps=ps, f1=f1: e.tensor_tensor(out=KB_[:, f1, :], in0=ps[:, 128:256], in1=rnorm[:],
                                                                        op=ALU.mult), reads=[psB, rnB], writes=[KAB])
                zv = zbf[:].rearrange("p (a b) -> p b a", b=64)
                for q8 in range(8):
                    p_, pB_ = npb()

                    def f(e, p_=p_, q8=q8):
                        for j in range(8):
                            ins = e.transpose(p_[0:64, j * 128:(j + 1) * 128], zv[:, q8 * 8 + j, :], ident[:])
                        return ins
                    T.op('pe', f, reads=[zbB, identB], writes=[pB_])
                    T.op('act', lambda e, p_=p_, q8=q8: e.activation(
                        out=zF[:, :, q8 * 8:(q8 + 1) * 8].rearrange("p c s -> p s c"),
                        in_=p_[0:64, :].rearrange("p (s c) -> p s c", c=128), func=AF.Copy), reads=[pB_], writes=[kfzB])
                step1(zF, 64, f1d)
                for f1 in range(65):
                    ps, psB = npf()

                    def f(e, ps=ps, f1=f1):
                        return step3_data(e, f1, ps)
                    T.op('pe', f, reads=[AB, tabB], writes=[psB])
                    ti_ = f1 % 2
                    T.op('dve', lambda e, ps=ps, f1=f1, ti_=ti_: e.tensor_tensor(out=tm[ti_][:, 0:128], in0=ps[:, 0:128],
                                                                                 in1=KA[:, f1, :], op=ALU.mult),
                         reads=[psB, KAB], writes=[tmB[ti_]])
                    T.op('dve', lambda e, ps=ps, f1=f1, ti_=ti_: e.tensor_tensor(out=tm[ti_][:, 128:256], in0=ps[:, 128:256],
                                                                                 in1=KB_[:, f1, :], op=ALU.mult),
                         reads=[psB, KAB], writes=[tmB[ti_]])
                    T.op('pool', lambda e, f1=f1, ti_=ti_: e.tensor_tensor(out=Y[:, :, f1], in0=tm[ti_][:, 0:128],
                                                                           in1=tm[ti_][:, 128:256], op=ALU.add),
                         reads=[tmB[ti_]], writes=[YB])
                for c4 in range(32):
                    ps, psB = npf()

                    def f(e, ps=ps, c4=c4):
                        for ci in range(4):
                            ins = e.matmul(ps[0:65, ci * 128:(ci + 1) * 128], lhsT=Y[:, c4 * 4 + ci, :], rhs=e64[:],
                                           start=True, stop=True)
                        return ins
                    T.op('pe', f, reads=[YB, tabB], writes=[psB])
                    src_ap = ps[0:65, :].rearrange("p (c r s) -> p r s c", r=2, s=64)
                    if c4 % 2 == 0:
                        T.op('act', lambda e, src_ap=src_ap, c4=c4: e.activation(out=CT[:, :, :, c4 * 4:(c4 + 1) * 4], in_=src_ap,
                                                                                 func=AF.Copy), reads=[psB], writes=[CTB])
                    else:
                        T.op('dve', lambda e, src_ap=src_ap, c4=c4: e.tensor_copy(out=CT[:, :, :, c4 * 4:(c4 + 1) * 4], in_=src_ap),
                             reads=[psB], writes=[CTB])
                for q4 in range(16):
                    ps, psB = npf()

                    def f(e, ps=ps, q4=q4):
                        for j in range(4):
                            s2 = q4 * 4 + j
                            e.matmul(ps[0:32, j * 128:(j + 1) * 128], lhsT=rt[:, s2, 0, :], rhs=CT[:, 0, s2, :], start=True, stop=False)
                            ins = e.matmul(ps[0:32, j * 128:(j + 1) * 128], lhsT=rt[:, s2, 1, :], rhs=CT[:, 1, s2, :],
                                           start=False, stop=True)
                        return ins
                    T.op('pe', f, reads=[CTB, tabB], writes=[psB])
                    T.op('act', lambda e, ps=ps, q4=q4: e.activation(out=ysb[:, q4 * 4:(q4 + 1) * 4, :],
                                                                     in_=ps[0:32, :].rearrange("p (s c) -> p s c", c=128),
                                                                     func=AF.Copy), reads=[psB], writes=[kfzB])
                yT3 = yT[:].rearrange("p (a b) -> p b a", b=64)
                for q2 in range(2):
                    p_, pB_ = npb()

                    def f(e, p_=p_, q2=q2):
                        for j in range(32):
                            ins = e.transpose(p_[:, j * 32:(j + 1) * 32], ysb[:, q2 * 32 + j, :], ident[0:32, 0:32])
                        return ins
                    T.op('pe', f, reads=[kfzB, identB], writes=[pB_])
                    T.op('dve', lambda e, p_=p_, q2=q2: e.tensor_copy(out=yT3[:, q2 * 32:(q2 + 1) * 32, :],
                                                                      in_=p_[:, :].rearrange("p (s a) -> p s a", a=32)),
                         reads=[pB_], writes=[yTB])
                T.op('dve', lambda e, g=g: e.scalar_tensor_tensor(out=yT[:], in0=zbf[:, 0:HALF], scalar=fbs[:, g:g + 1], in1=yT[:],
                                                                  op0=ALU.mult, op1=ALU.add), reads=[zbB, smB], writes=[yTB])
                T.op('pool', lambda e: e.tensor_tensor(out=yab[:], in0=yT[:], in1=x0c[:], op=ALU.mult),
                     reads=[yTB, x0B], writes=[yabB])
                T.dma('sp', yaT[c0:c0 + 128, :], yab[:], reads=[yabB], writes=[yaB])
            if DEBUG:
                T.barrier()
                T.dma('sp', dbg["d_ya"], yaT, reads=[yaB], writes=[Buf("dbg_ya")])
            T.barrier()
        sH.close()
        if STOP_AFTER <= 4:
            return nc, T

        ybB = Buf("ybT", dram=True)
        with ExitStack() as s5:
            s5.enter_context(nc.named_scope("s5"))
            kt = [sb("kt%d" % i, [128, NKEY], BF16, s5) for i in range(2)]; ktB = [Buf("kt%d" % i) for i in range(2)]
            vt = [sb("vt%d" % i, [128, 34, 128], BF16, s5) for i in range(2)]; vtB = [Buf("vtt%d" % i) for i in range(2)]
            qt = [sb("qt%d" % i, [128, 512], BF16, s5) for i in range(2)]; qtB = [Buf("qt%d" % i) for i in range(2)]
            pt = [sb("pt%d" % i, [128, 512], BF16, s5) for i in range(4)]; ptB = [Buf("pt%d" % i) for i in range(4)]
            acc = [sb("acc%d" % i, [128, 512], F32, s5) for i in range(4)]; accB = [Buf("acc%d" % i) for i in range(4)]
            rsm = sb("rsm", [128, 512], F32, s5); rsmB = Buf("rsm")
            ob = [sb("ob%d" % i, [128, 512], BF16, s5) for i in range(2)]; obB = [Buf("ob%d" % i) for i in range(2)]
            sc_i = [0]
            qi = 0; pi = 0
            cb5 = [sb("cb5_%d" % i, [128, 8192], BF16, s5) for i in range(2)]; cb5B = [Buf("cb5_%d" % i) for i in range(2)]

            def conv_chunks():
                it5 = 0
                for k in range(16):
                    i = it5 % 2; it5 += 1
                    T.dma('pool', cb5[i][:], w_ff1[k * 128:(k + 1) * 128, :], writes=[cb5B[i]])
                    T.dma('sp', wf1b[k * 128:(k + 1) * 128, :], cb5[i][:], reads=[cb5B[i]], writes=[wf1B])
                    yield
                for k in range(16):
                    i = it5 % 2; it5 += 1
                    T.dma('pool', cb5[i][:].rearrange("p (a c) -> p a c", a=4),
                          w_ff2[k * 512:(k + 1) * 512, :].rearrange("(a p) c -> p a c", p=128), writes=[cb5B[i]])
                    T.dma('sp', wf2b[k * 512:(k + 1) * 512, :].rearrange("(a p) c -> p a c", p=128),
                          cb5[i][:].rearrange("p (a c) -> p a c", a=4), reads=[cb5B[i]], writes=[wf2B])
                    yield
            conv_it = conv_chunks()
            for g in range(4):
                gi = g % 2
                T.dma('sp', kt[gi][:], kT[g], reads=[kTB], writes=[ktB[gi]])
                T.dma('sp', vt[gi][:], vtok[:, g * 128:(g + 1) * 128].rearrange("(j p) d -> p j d", p=128),
                      reads=[vB], writes=[vtB[gi]])
                for hh in range(4):
                    h = g * 4 + hh
                    for qb_ in range(4):
                        qq = qi % 2; qi += 1
                        next(conv_it, None)
                        T.dma('sp', qt[qq][:], qT[h, :, qb_ * 512:(qb_ + 1) * 512], reads=[qTB], writes=[qtB[qq]])
                        po_, poB = pf[3], pfB[3]
                        psm, psmB = pf[4], pfB[4]

                        def score(j, gi=gi, qq=qq):
                            si = sc_i[0] % 3; sc_i[0] += 1
                            ps, psB = pf[si], pfB[si]
                            T.op('pe', lambda e, ps=ps, j=j: e.matmul(
                                ps[:], lhsT=kt[gi][:, j * 128:(j + 1) * 128], rhs=qt[qq][:], start=True, stop=True),
                                reads=[ktB[gi], qtB[qq]], writes=[psB])
                            return ps, psB
                        nxt = score(0)
                        for j in range(34):
                            ps, psB = nxt
                            pp = pi % 4; pi += 1
                            T.op('act', lambda e, ps=ps, pp=pp: e.activation(out=pt[pp][:], in_=ps[:], func=AF.Exp,
                                                                             scale=128.0 ** -0.5),
                                 reads=[psB], writes=[ptB[pp]])
                            if j + 1 < 34:
                                nxt = score(j + 1)

                            T.op('pe', lambda e, pp=pp, gi=gi, j=j, po_=po_: e.matmul(
                                po_[:], lhsT=vt[gi][:, j, :], rhs=pt[pp][:], start=(j == 0), stop=(j == 33)),
                                reads=[ptB[pp], vtB[gi]], writes=[poB])
                            r3 = j % 3
                            if r3 == 2:
                                T.op('pe', lambda e, pp=pp, j=j, psm=psm: e.matmul(
                                    psm[:], lhsT=ones_b[:], rhs=pt[pp][:], start=(j == 2), stop=False),
                                    reads=[ptB[pp], onesbB], writes=[psmB])
                            else:
                                ai = r3 * 2 + ((j // 3) % 2)
                                aeng = 'dve' if r3 == 0 else 'pool'
                                if j // 3 < 2:
                                    T.op(aeng, lambda e, pp=pp, ai=ai: e.tensor_copy(out=acc[ai][:], in_=pt[pp][:]),
                                         reads=[ptB[pp]], writes=[accB[ai]])
                                else:
                                    T.op(aeng, lambda e, pp=pp, ai=ai: e.tensor_tensor(out=acc[ai][:], in0=acc[ai][:], in1=pt[pp][:],
                                                                                       op=ALU.add), reads=[ptB[pp]], writes=[accB[ai]])

                        def fs(e, psm=psm):
                            for ai in range(4):
                                ins = e.matmul(psm[:], lhsT=ones_f[:], rhs=acc[ai][:], start=False, stop=(ai == 3))
                            return ins
                        T.op('pe', fs, reads=accB + [onesB], writes=[psmB])
                        T.op('dve', lambda e, psm=psm: e.reciprocal(out=rsm[:], in_=psm[:]), reads=[psmB], writes=[rsmB])
                        oo = qi % 2
                        T.op('dve', lambda e, po_=po_, oo=oo: e.tensor_tensor(out=ob[oo][:], in0=po_[:], in1=rsm[:], op=ALU.mult),
                             reads=[poB, rsmB], writes=[obB[oo]])
                        T.dma('sp', ybT[h * 128:(h + 1) * 128, qb_ * 512:(qb_ + 1) * 512], ob[oo][:], reads=[obB[oo]],
                              writes=[ybB])
            if DEBUG:
                T.barrier()
                T.dma('sp', dbg["d_yb"], ybT, reads=[ybB], writes=[Buf("dbg_yb")])
            T.barrier()
        if STOP_AFTER <= 5:
            return nc, T

        x1B = Buf("x1s", dram=True)
        with ExitStack() as s6:
            s6.enter_context(nc.named_scope("s6"))
            ya_s = sb("ya_s", [128, 8, 512], BF16, s6); yaSB = Buf("ya_s")
            yb_s = sb("yb_s", [128, 16, 512], BF16, s6); ybSB = Buf("yb_s")
            mg = sb("mg", [128, 16, 512], BF16, s6); mgB = Buf("mg")
            wa_ = [sb("wm_a%d" % i, [128, 8, 512], BF16, s6) for i in range(2)]; waB_ = [Buf("wm_a%d" % i) for i in range(2)]
            wb_ = [sb("wm_b%d" % i, [128, 16, 512], BF16, s6) for i in range(2)]; wbB_ = [Buf("wm_b%d" % i) for i in range(2)]
            gg = [sb("gg%d" % i, [128, 2, 512], BF16, s6) for i in range(2)]; ggB = [Buf("gg%d" % i) for i in range(2)]
            t1 = [sb("t1_%d" % i, [128, 512], F32, s6) for i in range(2)]; t1B = [Buf("t1_%d" % i) for i in range(2)]
            wo_ = [sb("wo%d" % i, [128, 16, 512], BF16, s6) for i in range(2)]; woB = [Buf("wo%d" % i) for i in range(2)]
            mix = [sb("mix%d" % i, [128, D], F32, s6) for i in range(4)]; mixB = [Buf("mix%d" % i) for i in range(4)]
            xo = [sb("xo%d" % i, [128, D], F32, s6) for i in range(2)]; xoB = [Buf("xo%d" % i) for i in range(2)]
            jk = sb("jk6", [128, D], BF16, s6); jkB = Buf("jk6")
            st6 = [sb("st6_%d" % i, [128, 4], F32, s6) for i in range(2)]; st6B = [Buf("st6_%d" % i) for i in range(2)]
            wi = 0
            for tb in range(4):
                ts_ = slice(tb * 512, (tb + 1) * 512)
                T.dma('sp', ya_s[:], yaT[:, ts_].rearrange("(k p) t -> p k t", p=128), reads=[yaB], writes=[yaSB])
                T.dma('sp', yb_s[:], ybT[:, ts_].rearrange("(k p) t -> p k t", p=128), reads=[ybB], writes=[ybSB])
                for oc4 in range(4):
                    i = wi % 2; wi += 1
                    T.dma('sp', wa_[i][:], wbab[:, oc4 * 512:(oc4 + 1) * 512].rearrange("(k p) c -> p k c", p=128), reads=[wmB], writes=[waB_[i]])
                    T.dma('sp', wb_[i][:], wbbb[:, oc4 * 512:(oc4 + 1) * 512].rearrange("(k p) c -> p k c", p=128), reads=[wmB], writes=[wbB_[i]])
                    for q in range(4):
                        oc = oc4 * 4 + q
                        gi_ = oc % 2
                        T.dma('sp', gg[gi_][:, 0, :], gT[oc * 128:(oc + 1) * 128, ts_], reads=[gTB], writes=[ggB[gi_]])
                        T.dma('sp', gg[gi_][:, 1, :], gT[D + oc * 128:D + (oc + 1) * 128, ts_], reads=[gTB], writes=[ggB[gi_]])
                        pa, paB = npf()

                        def fa(e, pa=pa, i=i, q=q):
                            for k in range(8):
                                ins = e.matmul(pa[:], lhsT=wa_[i][:, k, q * 128:(q + 1) * 128], rhs=ya_s[:, k, :], start=(k == 0), stop=(k == 7))
                            return ins
                        T.op('pe', fa, reads=[waB_[i], yaSB], writes=[paB])
                        pb2, pb2B = npf()

                        def fb(e, pb2=pb2, i=i, q=q):
                            for k in range(16):
                                ins = e.matmul(pb2[:], lhsT=wb_[i][:, k, q * 128:(q + 1) * 128], rhs=yb_s[:, k, :], start=(k == 0), stop=(k == 15))
                            return ins
                        T.op('pe', fb, reads=[wbB_[i], ybSB], writes=[pb2B])
                        T.op('dve', lambda e, pa=pa, gi_=gi_: e.tensor_tensor(out=t1[gi_][:], in0=pa[:], in1=gg[gi_][:, 0, :], op=ALU.mult),
                             reads=[paB, ggB[gi_]], writes=[t1B[gi_]])
                        T.op('dve', lambda e, pb2=pb2, gi_=gi_: e.tensor_tensor(out=gg[gi_][:, 1, :], in0=pb2[:], in1=gg[gi_][:, 1, :], op=ALU.mult),
                             reads=[pb2B], writes=[ggB[gi_]])
                        T.op('pool', lambda e, gi_=gi_, oc=oc: e.tensor_tensor(out=mg[:, oc, :], in0=t1[gi_][:], in1=gg[gi_][:, 1, :], op=ALU.add),
                             reads=[t1B[gi_], ggB[gi_]], writes=[mgB])
                for cb4 in range(4):
                    i = wi % 2; wi += 1
                    T.dma('sp', wo_[i][:], wob[:, cb4 * 512:(cb4 + 1) * 512].rearrange("(k p) c -> p k c", p=128), reads=[wmB], writes=[woB[i]])
                    for tl in range(4):
                        ps, psB = npf()

                        def fo(e, ps=ps, i=i, tl=tl):
                            for k in range(16):
                                ins = e.matmul(ps[:], lhsT=mg[:, k, tl * 128:(tl + 1) * 128], rhs=wo_[i][:, k, :],
                                               start=(k == 0), stop=(k == 15))
                            return ins
                        T.op('pe', fo, reads=[woB[i], mgB], writes=[psB])
                        T.op('act', lambda e, ps=ps, tl=tl, cb4=cb4: e.activation(out=mix[tl][:, cb4 * 512:(cb4 + 1) * 512], in_=ps[:],
                                                                                  func=AF.Copy), reads=[psB], writes=[mixB[tl]])
                for tl in range(4):
                    i = tl % 2
                    r0 = tb * 512 + tl * 128
                    T.dma('sp', xo[i][:], xp[r0:r0 + 128, :], writes=[xoB[i]])
                    T.op('act', lambda e, tl=tl, i=i: e.activation(out=jk[:], in_=mix[tl][:], func=AF.Square, accum_out=st6[i][:, 0:1]),
                         reads=[mixB[tl]], writes=[jkB, st6B[i]])
                    T.op('act', lambda e, i=i: e.activation(out=st6[i][:, 1:2], in_=st6[i][:, 0:1], func=AF.Sqrt, bias=epsT[:, 0:1],
                                                            scale=1.0 / D), reads=[epsB], writes=[st6B[i]])
                    T.op('dve', lambda e, i=i: e.reciprocal(out=st6[i][:, 2:3], in_=st6[i][:, 1:2]), writes=[st6B[i]])
                    T.op('dve', lambda e, tl=tl, i=i: e.scalar_tensor_tensor(out=mix[tl][:], in0=mix[tl][:], scalar=st6[i][:, 2:3],
                                                                             in1=Grow[:, 0, :], op0=ALU.mult, op1=ALU.mult),
                         reads=[st6B[i], GrowB], writes=[mixB[tl]])
                    T.op('pool', lambda e, tl=tl, i=i: e.tensor_tensor(out=xo[i][:], in0=xo[i][:], in1=mix[tl][:], op=ALU.add),
                         reads=[mixB[tl]], writes=[xoB[i]])
                    T.dma('sp', x1s[r0:r0 + 128, :], xo[i][:], reads=[xoB[i]], writes=[x1B])
            if DEBUG:
                T.barrier()
                T.dma('sp', dbg["d_x1"], x1s, reads=[x1B], writes=[Buf("dbg_x1")])
            T.barrier()
        if STOP_AFTER <= 6:
            return nc, T

        outB = Buf("out", dram=True)
        with ExitStack() as s7:
            s7.enter_context(nc.named_scope("s7"))
            xt7 = [sb("x7_%d" % i, [128, D], F32, s7) for i in range(2)]; xt7B = [Buf("x7_%d" % i) for i in range(2)]
            xn7 = [sb("xn7_%d" % i, [128, D], BF16, s7) for i in range(2)]; xn7B = [Buf("xn7_%d" % i) for i in range(2)]
            jk7 = sb("jk7", [128, D], BF16, s7); jk7B = Buf("jk7")
            st7 = [sb("st7_%d" % i, [128, 4], F32, s7) for i in range(4)]; st7B = [Buf("st7_%d" % i) for i in range(4)]
            h2T = sb("h2T", [128, 16, 512], BF16, s7); h2B_ = Buf("h2T")
            uT7 = sb("uT7", [128, 64, 512], BF16, s7); u7B = Buf("uT7")
            w1s = [sb("w1s%d" % i, [128, 16, 128], BF16, s7) for i in range(3)]; w1sB = [Buf("w1s%d" % i) for i in range(3)]
            w2s = [sb("w2s%d" % i, [128, 16, 512], BF16, s7) for i in range(2)]; w2sB = [Buf("w2s%d" % i) for i in range(2)]
            fo7 = [sb("fo7_%d" % i, [128, D], F32, s7) for i in range(4)]; fo7B = [Buf("fo7_%d" % i) for i in range(4)]
            rl7 = [sb("rl7_%d" % i, [128, 512], F32, s7) for i in range(2)]; rl7B = [Buf("rl7_%d" % i) for i in range(2)]
            w1i = 0; w2i = 0
            for tb in range(4):
                for tl in range(4):
                    r0 = tb * 512 + tl * 128
                    i2 = tl % 2
                    T.dma('sp', xt7[i2][:], x1s[r0:r0 + 128, :], reads=[x1B], writes=[xt7B[i2]])
                    T.op('act', lambda e, tl=tl, i2=i2: e.activation(out=jk7[:], in_=xt7[i2][:], func=AF.Square, accum_out=st7[tl][:, 0:1]),
                         reads=[xt7B[i2]], writes=[jk7B, st7B[tl]])
                    T.op('act', lambda e, tl=tl: e.activation(out=st7[tl][:, 1:2], in_=st7[tl][:, 0:1], func=AF.Sqrt, bias=epsT[:, 0:1],
                                                              scale=1.0 / D), reads=[epsB], writes=[st7B[tl]])
                    T.op('dve', lambda e, tl=tl: e.reciprocal(out=st7[tl][:, 2:3], in_=st7[tl][:, 1:2]), writes=[st7B[tl]])
                    T.op('dve', lambda e, tl=tl, i2=i2: e.tensor_scalar(out=xn7[i2][:], in0=xt7[i2][:], scalar1=st7[tl][:, 2:3],
                                                                        scalar2=None, op0=ALU.mult),
                         reads=[xt7B[i2], st7B[tl]], writes=[xn7B[i2]])
                    for hb in range(2):
                        p_, pB_ = npb()

                        def f(e, p_=p_, i2=i2, hb=hb):
                            for j in range(8):
                                k = hb * 8 + j
                                ins = e.transpose(p_[:, j * 128:(j + 1) * 128], xn7[i2][:, k * 128:(k + 1) * 128], ident[:])
                            return ins
                        T.op('pe', f, reads=[xn7B[i2], identB], writes=[pB_])
                        for j in range(8):
                            k = hb * 8 + j
                            T.op('act', lambda e, p_=p_, j=j, k=k, tl=tl: e.activation(
                                out=h2T[:, k, tl * 128:(tl + 1) * 128], in_=p_[:, j * 128:(j + 1) * 128], func=AF.Identity,
                                bias=modv[:, 5, k:k + 1], scale=modv[:, 4, k:k + 1]), reads=[pB_, modB], writes=[h2B_])
                for fc in range(64):
                    i = w1i % 3; w1i += 1
                    T.dma('sp', w1s[i][:], wf1b[:, fc * 128:(fc + 1) * 128].rearrange("(k p) c -> p k c", p=128),
                          reads=[wf1B], writes=[w1sB[i]])
                    ps, psB = npf()

                    def f(e, ps=ps, i=i):
                        for k in range(16):
                            ins = e.matmul(ps[:], lhsT=w1s[i][:, k, :], rhs=h2T[:, k, :], start=(k == 0), stop=(k == 15))
                        return ins
                    T.op('pe', f, reads=[w1sB[i], h2B_], writes=[psB])
                    ri_ = fc % 2
                    T.op('act', lambda e, ps=ps, ri_=ri_: e.activation(out=rl7[ri_][:], in_=ps[:], func=AF.Relu),
                         reads=[psB], writes=[rl7B[ri_]])
                    T.op('dve', lambda e, fc=fc, ri_=ri_: e.tensor_tensor(out=uT7[:, fc, :], in0=rl7[ri_][:], in1=rl7[ri_][:],
                                                                          op=ALU.mult), reads=[rl7B[ri_]], writes=[u7B])
                for cb4 in range(4):
                    accs = [npf() for _ in range(4)]
                    for kp in range(4):
                        i = w2i % 2; w2i += 1
                        T.dma('sp', w2s[i][:], wf2b[kp * 2048:(kp + 1) * 2048, cb4 * 512:(cb4 + 1) * 512].rearrange(
                            "(k p) c -> p k c", p=128), reads=[wf2B], writes=[w2sB[i]])
                        for tl in range(4):
                            ps, psB = accs[tl]

                            def f(e, ps=ps, i=i, tl=tl, kp=kp):
                                for k in range(16):
                                    ins = e.matmul(ps[:], lhsT=uT7[:, kp * 16 + k, tl * 128:(tl + 1) * 128], rhs=w2s[i][:, k, :],
                                                   start=(kp == 0 and k == 0), stop=(kp == 3 and k == 15))
                                return ins
                            T.op('pe', f, reads=[w2sB[i], u7B], writes=[psB])
                    for tl in range(4):
                        ps, psB = accs[tl]
                        T.op('act', lambda e, ps=ps, tl=tl, cb4=cb4: e.activation(out=fo7[tl][:, cb4 * 512:(cb4 + 1) * 512], in_=ps[:],
                                                                                  func=AF.Copy), reads=[psB], writes=[fo7B[tl]])
                for tl in range(4):
                    r0 = tb * 512 + tl * 128
                    i2 = tl % 2
                    T.dma('sp', xt7[i2][:], x1s[r0:r0 + 128, :], reads=[x1B], writes=[xt7B[i2]])
                    T.op('act', lambda e, tl=tl: e.activation(out=jk7[:], in_=fo7[tl][:], func=AF.Square, accum_out=st7[tl][:, 0:1]),
                         reads=[fo7B[tl]], writes=[jk7B, st7B[tl]])
                    T.op('act', lambda e, tl=tl: e.activation(out=st7[tl][:, 1:2], in_=st7[tl][:, 0:1], func=AF.Sqrt, bias=epsT[:, 0:1],
                                                              scale=1.0 / D), reads=[epsB], writes=[st7B[tl]])
                    T.op('dve', lambda e, tl=tl: e.reciprocal(out=st7[tl][:, 2:3], in_=st7[tl][:, 1:2]), writes=[st7B[tl]])
                    T.op('dve', lambda e, tl=tl: e.scalar_tensor_tensor(out=fo7[tl][:], in0=fo7[tl][:], scalar=st7[tl][:, 2:3],
                                                                        in1=Grow[:, 1, :], op0=ALU.mult, op1=ALU.mult),
                         reads=[st7B[tl], GrowB], writes=[fo7B[tl]])
                    T.op('pool', lambda e, tl=tl, i2=i2: e.tensor_tensor(out=fo7[tl][:], in0=fo7[tl][:], in1=xt7[i2][:], op=ALU.add),
                         reads=[xt7B[i2]], writes=[fo7B[tl]])
                    T.dma('sp', out[r0:r0 + 128, :], fo7[tl][:], reads=[fo7B[tl]], writes=[outB])
            T.barrier()
        return nc, T


def host_consts():
    bf = ml_dtypes.bfloat16
    t = {}
    slot = np.arange(2 * L)
    d = np.where(slot < L, slot, 2 * L - slot).astype(np.float64)
    tt = d / (L - 1)
    wpos = 2 * np.pi * d / L
    bands = np.linspace(1e-4, 7, 8)
    emb = np.concatenate([tt[:, None], np.cos(bands * wpos[:, None]), -np.sin(bands * wpos[:, None])], axis=1)
    t["embT"] = np.ascontiguousarray(emb.T).astype(np.float32)
    t["negt"] = np.ascontiguousarray((-tt).reshape(128, 64)).astype(np.float32)
    MIN_DECAY = math.log(1e-2) / 1.5; MAX_DECAY = math.log(1e-2) / 0.3
    dl = np.abs(np.linspace(MIN_DECAY, MAX_DECAY, C, dtype=np.float32))
    t["delta"] = np.ascontiguousarray(np.broadcast_to(dl, (128, C))).astype(np.float32)
    s1 = np.arange(128)[:, None]; f1 = np.arange(65)[None, :]
    th = 2 * np.pi * s1 * f1 / 128
    t["F1k"] = np.concatenate([np.cos(th), -np.sin(th)], axis=1).astype(bf)
    s2 = np.arange(64)[:, None, None]; ff1 = np.arange(65)[None, :, None]; f2 = np.arange(64)[None, None, :]
    th = 2 * np.pi * s2 * (ff1 + 128 * f2) / 8192.0
    Mr, Mi = np.cos(th), -np.sin(th)
    P_ = np.concatenate([Mr, -Mi], axis=0); Q_ = np.concatenate([Mi, Mr], axis=0)
    t["T3"] = np.stack([P_, Q_, P_, -Q_], axis=2).astype(bf)
    f2 = np.arange(64)[:, None]; s2 = np.arange(64)[None, :]
    th = 2 * np.pi * f2 * s2 / 64
    E = np.zeros((2, 64, 2, 64))
    E[0, :, 0, :] = np.cos(th); E[1, :, 0, :] = -np.sin(th); E[0, :, 1, :] = np.sin(th); E[1, :, 1, :] = np.cos(th)
    t["E64"] = E.reshape(128, 128).astype(bf)
    t["ident"] = np.eye(128).astype(bf)
    return t


def host_tables(hf, consts):
    bf = ml_dtypes.bfloat16
    t = {}
    pos = (np.arange(L) + hf * HALF) % L
    rows = (pos // 64).astype(np.float32); cols = (pos % 64).astype(np.float32)
    inv = (10000.0 ** (-np.arange(0, 64, 2, dtype=np.float32) / 64)).astype(np.float32)
    ang = np.stack([rows[:, None] * inv, cols[:, None] * inv], axis=1)
    c_, s_ = np.cos(ang), np.sin(ang)
    C128 = np.concatenate([c_[:, 0], c_[:, 0], c_[:, 1], c_[:, 1]], axis=1)
    S128 = np.concatenate([-s_[:, 0], s_[:, 0], -s_[:, 1], s_[:, 1]], axis=1)
    t["ropec"] = np.ascontiguousarray(C128.reshape(32, 128, 128).transpose(1, 0, 2)).astype(bf)
    t["ropes"] = np.ascontiguousarray(S128.reshape(32, 128, 128).transpose(1, 0, 2)).astype(bf)
    s1t = (np.arange(64) + 32 * hf) % 64
    t["F1d"] = np.ascontiguousarray(consts["F1k"][s1t, :])
    f1 = np.arange(65)[:, None, None]; s2 = np.arange(64)[None, :, None]; s1 = (np.arange(32) + 32 * hf)[None, None, :]
    ph = 2 * np.pi * f1 * (64 * s1 + s2) / 8192.0
    w = np.where((f1 == 0) | (f1 == 64), 1.0, 2.0) / 8192.0
    t["Rt"] = np.stack([w * np.cos(ph), -w * np.sin(ph)], axis=2).astype(bf)
    t["halo"] = np.ascontiguousarray(np.broadcast_to(np.array([hf, 1 - hf], np.float32), (128, 2)))
    return t


_NC = [None]


def make_in_maps(inputs):
    inp = {k: np.asarray(v) for k, v in inputs.items()}
    consts = host_consts()
    f32 = np.float32
    col = lambda v, k: np.ascontiguousarray(np.asarray(v, f32).reshape(k, 128).T)
    shared = dict(consts)
    shared["w_ada"] = inp["w_ada"][0]; shared["w_in"] = inp["w_in"][0]
    shared["bada"] = col(inp["b_ada"][0], 96)
    ng = inp["norm_gains"][0]; ba = inp["b_ada"][0]
    rowb = np.stack([ba[2 * D:3 * D], ba[5 * D:6 * D], ng[1], ng[3]], axis=0)
    shared["rowb"] = np.ascontiguousarray(np.broadcast_to(rowb[None], (128, 4, D))).astype(f32)
    shared["gpre"] = np.ascontiguousarray(np.stack([col(ng[0], 16), col(ng[2], 16)], axis=1))
    cw = inp["conv_w"][0]
    shared["convw"] = np.ascontiguousarray(np.stack([col(cw[j], 24) for j in range(3)], axis=2))
    shared["convb"] = col(inp["conv_b"][0], 24)
    shared["fw1"] = inp["filt_w1"][0]; shared["fw2"] = inp["filt_w2"][0]
    shared["fw3d"] = np.ascontiguousarray(np.concatenate([inp["filt_w3"][0]] * 2, axis=1))
    fv = np.stack([inp["filt_b1"][0], inp["filt_b2"][0], inp["filt_b3"][0], inp["filt_freq"][0]], axis=1)
    shared["fvec"] = np.ascontiguousarray(np.concatenate([fv, fv], axis=0)).astype(f32)
    w4 = inp["filt_w4"][0]
    shared["fw4"] = np.ascontiguousarray(np.concatenate([w4[:, :C], w4[:, C:]], axis=0))
    shared["fbias"] = col(inp["filt_bias"][0], 8)
    shared["qkg"] = np.ascontiguousarray(np.broadcast_to(inp["qk_gains"][0][None], (128, 2, 128))).astype(f32)
    shared["w_ba"] = inp["w_branch_a"][0]; shared["w_bb"] = inp["w_branch_b"][0]; shared["w_o"] = inp["w_out"][0]
    shared["w_ff1"] = inp["w_ff1"][0]; shared["w_ff2"] = inp["w_ff2"][0]
    per_hf = [host_tables(hf, consts) for hf in range(2)]
    in_maps = []
    for core in range(8):
        b, hf = core // 2, core % 2
        m = dict(shared); m.update(per_hf[hf])
        for nm in ("w_ada", "w_in", "w_ba", "w_bb", "w_o", "w_ff1", "w_ff2", "rowb", "T3"):
            a = shared[nm]
            pad = np.full((1,) + a.shape[1:], core, dtype=a.dtype)
            m[nm] = np.concatenate([a, pad], axis=0)
        m["xp"] = np.ascontiguousarray(np.roll(inp["x"][b], -hf * HALF, axis=0))
        m["ctx"] = np.ascontiguousarray(inp["ctx"][b])
        m["cvec"] = np.ascontiguousarray(np.stack([col(inp["c"][b], 16), col(inp["c_ctx"], 16)], axis=2))
        in_maps.append(m)
    return in_maps


def kernel(**inputs):
    if _NC[0] is None:
        _NC[0] = build_nc()[0]
    nc = _NC[0]
    in_maps = make_in_maps(inputs)
    res = run_bass_kernel_spmd(nc, in_maps, core_ids=list(range(8)))
    out = np.empty((NB, L, D), np.float32)
    for core in range(8):
        b, hf = core // 2, core % 2
        out[b, hf * HALF:(hf + 1) * HALF] = res.results[core]["out"]
    return out
```
